# Optimizing a Trainium2 kernel written in Bass

```python
import math
import jax, jax.numpy as jnp
from jax import lax
import numpy as np

D_MODEL = 2048
BATCH = 2
SEQ = 16384
DEPTH = 1

D_MIX = D_MODEL
HEAD_DIM = 128
N_Q_HEADS = 8
N_KV_HEADS = 2
GROUP = N_Q_HEADS // N_KV_HEADS
ATTN_W = N_Q_HEADS * HEAD_DIM
LRU_W = D_MIX - ATTN_W
LRU_BLOCKS = 8
LRU_BW = LRU_W // LRU_BLOCKS
LRU_C = 8.0
CONV_W = 4
CONV_LEFT = 2
D_FF = 5632
PLE_DIM = 256
GRID_W = 64
ROPE_THETA = 10000.0
AXIS_DIM = HEAD_DIM // 2
Q_BLOCK = 128
EPS = 1e-6
IN_COLS = ATTN_W + 2 * N_KV_HEADS * HEAD_DIM + 2 * LRU_W

kernel_name = "hymba_style_bidir_attn_rglru_macaron_layer"


def rms_norm(x, g):
    xf = x.astype(jnp.float32)
    y = xf * lax.rsqrt(jnp.mean(xf * xf, axis=-1, keepdims=True) + EPS)
    return (y * g.astype(jnp.float32)).astype(x.dtype)


def swiglu(h, w1, w3, w2):
    return (jax.nn.silu(h @ w1) * (h @ w3)) @ w2


def axial_rope_tables(seq_len, dtype):
    rows = seq_len // GRID_W
    r = jnp.repeat(jnp.arange(rows, dtype=jnp.float32), GRID_W)
    c = jnp.tile(jnp.arange(GRID_W, dtype=jnp.float32), rows)
    inv = ROPE_THETA ** (-jnp.arange(0, AXIS_DIM, 2, dtype=jnp.float32) / AXIS_DIM)
    ang = jnp.concatenate([r[:, None] * inv, c[:, None] * inv], axis=-1)
    return jnp.cos(ang).astype(dtype), jnp.sin(ang).astype(dtype)


def apply_rope(x, cos, sin):
    x1 = x[..., 0::2]
    x2 = x[..., 1::2]
    c = cos[None, :, None, :]
    s = sin[None, :, None, :]
    out = jnp.stack([x1 * c - x2 * s, x1 * s + x2 * c], axis=-1)
    return out.reshape(x.shape)


def bidir_gqa(q, k, v):
    b, s = q.shape[0], q.shape[1]
    nb = s // Q_BLOCK
    scale = 1.0 / math.sqrt(HEAD_DIM)
    qb = q.reshape(b, nb, Q_BLOCK, N_KV_HEADS, GROUP, HEAD_DIM).transpose(1, 0, 2, 3, 4, 5)

    def one_block(qblk):
        sc = jnp.einsum('bqkgd,bskd->bkgqs', qblk, k).astype(jnp.float32) * scale
        pr = jax.nn.softmax(sc, axis=-1).astype(v.dtype)
        return jnp.einsum('bkgqs,bskd->bqkgd', pr, v)

    o = lax.map(one_block, qb)
    return o.transpose(1, 0, 2, 3, 4, 5).reshape(b, s, ATTN_W)


def centred_dw_conv(u, w, bias):
    s = u.shape[1]
    up = jnp.pad(u, ((0, 0), (CONV_LEFT, CONV_W - 1 - CONV_LEFT), (0, 0)))
    out = bias
    for j in range(CONV_W):
        out = out + up[:, j:j + s] * w[j]
    return out


def _lin_combine(e1, e2):
    a1, b1 = e1
    a2, b2 = e2
    return a1 * a2, a2 * b1 + b2


def rglru(u, w_a, b_a, w_i, b_i, lam, reverse):
    b, s, _ = u.shape
    ub = u.reshape(b, s, LRU_BLOCKS, LRU_BW)
    r = jax.nn.sigmoid((jnp.einsum('bshi,hij->bshj', ub, w_a) + b_a).astype(jnp.float32)).reshape(b, s, LRU_W)
    i = jax.nn.sigmoid((jnp.einsum('bshi,hij->bshj', ub, w_i) + b_i).astype(jnp.float32)).reshape(b, s, LRU_W)
    log_a = -LRU_C * jax.nn.softplus(-lam.astype(jnp.float32)) * r
    a = jnp.exp(log_a)
    mult = jnp.sqrt(-jnp.expm1(2.0 * log_a))
    bx = mult * i * u.astype(jnp.float32)
    _, h = lax.associative_scan(_lin_combine, (a, bx), axis=1, reverse=reverse)
    return h.astype(u.dtype)


def setup_inputs(seed: int = 0) -> dict:
    key = jax.random.key(seed)
    ks = jax.random.split(key, 32)
    f32 = jnp.float32

    def w(k, shape, fan_in, gain=1.0):
        return jax.random.normal(k, shape, f32) * (gain * fan_in ** -0.5)

    def gain(k, shape):
        return jnp.ones(shape, f32) + 0.01 * jax.random.normal(k, shape, f32)

    a0 = jax.random.uniform(ks[20], (DEPTH, 2, LRU_W), f32, 0.9, 0.999)
    sg = a0 ** (1.0 / LRU_C)
    lam = jnp.log(sg) - jnp.log1p(-sg)

    return {
        "x": jax.random.normal(ks[0], (BATCH, SEQ, D_MODEL), f32),
        "p": jax.random.normal(ks[1], (DEPTH, BATCH, SEQ, PLE_DIM), f32),
        "norm_ffn1": gain(ks[2], (DEPTH, D_MODEL)),
        "w1_ffn1": w(ks[3], (DEPTH, D_MODEL, D_FF), D_MODEL),
        "w3_ffn1": w(ks[4], (DEPTH, D_MODEL, D_FF), D_MODEL),
        "w2_ffn1": w(ks[5], (DEPTH, D_FF, D_MODEL), D_FF),
        "norm_mix": gain(ks[6], (DEPTH, D_MODEL)),
        "w_in": w(ks[7], (DEPTH, D_MODEL, IN_COLS), D_MODEL),
        "q_norm": gain(ks[8], (DEPTH, HEAD_DIM)),
        "k_norm": gain(ks[9], (DEPTH, HEAD_DIM)),
        "conv_w": w(ks[10], (DEPTH, CONV_W, LRU_W), CONV_W),
        "conv_b": 0.01 * jax.random.normal(ks[11], (DEPTH, LRU_W), f32),
        "lru_wa": w(ks[12], (DEPTH, 2, LRU_BLOCKS, LRU_BW, LRU_BW), LRU_BW),
        "lru_ba": 0.01 * jax.random.normal(ks[13], (DEPTH, 2, LRU_BLOCKS, LRU_BW), f32),
        "lru_wi": w(ks[14], (DEPTH, 2, LRU_BLOCKS, LRU_BW, LRU_BW), LRU_BW),
        "lru_bi": 0.01 * jax.random.normal(ks[15], (DEPTH, 2, LRU_BLOCKS, LRU_BW), f32),
        "lru_lambda": lam,
        "w_out": w(ks[16], (DEPTH, D_MIX, D_MODEL), D_MIX),
        "norm_ffn2": gain(ks[17], (DEPTH, D_MODEL)),
        "w1_ffn2": w(ks[18], (DEPTH, D_MODEL, D_FF), D_MODEL),
        "w3_ffn2": w(ks[19], (DEPTH, D_MODEL, D_FF), D_MODEL),
        "w2_ffn2": w(ks[21], (DEPTH, D_FF, D_MODEL), D_FF),
        "norm_ple": gain(ks[22], (DEPTH, D_MODEL)),
        "w_ple_gate": w(ks[23], (DEPTH, D_MODEL, D_MODEL), D_MODEL),
        "w_ple_proj": w(ks[24], (DEPTH, PLE_DIM, D_MODEL), PLE_DIM),
        "norm_final": gain(ks[25], (D_MODEL,)),
    }


def reference(x, p, norm_ffn1, w1_ffn1, w3_ffn1, w2_ffn1, norm_mix, w_in, q_norm, k_norm,
              conv_w, conv_b, lru_wa, lru_ba, lru_wi, lru_bi, lru_lambda, w_out,
              norm_ffn2, w1_ffn2, w3_ffn2, w2_ffn2, norm_ple, w_ple_gate, w_ple_proj,
              norm_final):
    b, s, _ = x.shape
    cos, sin = axial_rope_tables(s, x.dtype)
    kv_w = N_KV_HEADS * HEAD_DIM
    for l in range(DEPTH):
        x = x + 0.5 * swiglu(rms_norm(x, norm_ffn1[l]), w1_ffn1[l], w3_ffn1[l], w2_ffn1[l])

        h = rms_norm(x, norm_mix[l])
        proj = h @ w_in[l]
        q = proj[..., :ATTN_W].reshape(b, s, N_Q_HEADS, HEAD_DIM)
        k = proj[..., ATTN_W:ATTN_W + kv_w].reshape(b, s, N_KV_HEADS, HEAD_DIM)
        v = proj[..., ATTN_W + kv_w:ATTN_W + 2 * kv_w].reshape(b, s, N_KV_HEADS, HEAD_DIM)
        u = proj[..., ATTN_W + 2 * kv_w:ATTN_W + 2 * kv_w + LRU_W]
        y = proj[..., ATTN_W + 2 * kv_w + LRU_W:]

        q = apply_rope(rms_norm(q, q_norm[l]), cos, sin)
        k = apply_rope(rms_norm(k, k_norm[l]), cos, sin)
        attn_out = bidir_gqa(q, k, v)

        uc = centred_dw_conv(u, conv_w[l], conv_b[l])
        h_f = rglru(uc, lru_wa[l, 0], lru_ba[l, 0], lru_wi[l, 0], lru_bi[l, 0], lru_lambda[l, 0], False)
        h_b = rglru(uc, lru_wa[l, 1], lru_ba[l, 1], lru_wi[l, 1], lru_bi[l, 1], lru_lambda[l, 1], True)
        lru_out = (h_f + h_b) * jax.nn.gelu(y)

        mixed = jnp.concatenate([attn_out, lru_out], axis=-1)
        x = x + mixed @ w_out[l]

        x = x + 0.5 * swiglu(rms_norm(x, norm_ffn2[l]), w1_ffn2[l], w3_ffn2[l], w2_ffn2[l])

        gate = jax.nn.sigmoid(rms_norm(x, norm_ple[l]) @ w_ple_gate[l])
        x = x + gate * (p[l] @ w_ple_proj[l])
    return rms_norm(x, norm_final)
```

```python
import math
from contextlib import ExitStack

import numpy as np
import concourse.bass as bass
import concourse.mybir as mybir
from concourse.bass_utils import run_bass_kernel_spmd

F32 = mybir.dt.float32
BF16 = mybir.dt.bfloat16
AF = mybir.ActivationFunctionType
ALU = mybir.AluOpType
AX = mybir.AxisListType

D = 2048
DFF = 5632
NTOK = 4096
TT = 512
NTILE = NTOK // TT
SEQ = 16384
EPS = 1e-6
NS = 168
C_G1, C_GM, C_G2, C_GP = 0, 16, 32, 48
C_CW, C_CB, C_BA, C_BI, C_LAM, C_MASK = 64, 96, 104, 120, 136, 152
GELU_K = 2.0 * math.sqrt(2.0 / math.pi)
DBG = False
NO_CC = False
NKC = 128
PHASES = 'ACDE'
CUT = 0


class StopEmit(Exception):
    pass


def cp(k):
    if CUT == k:
        raise StopEmit()


class Tok:
    __slots__ = ("sem", "n")

    def __init__(self, sem, n):
        self.sem = sem
        self.n = n


class Res:
    def __init__(self):
        self.t = {}

    def add(self, tok):
        if tok is None:
            return
        k = id(tok.sem)
        if k not in self.t or self.t[k].n < tok.n:
            self.t[k] = tok

    def set(self, tok):
        self.t = {}
        self.add(tok)


class DSem:
    def __init__(self, sem):
        self.sem = sem
        self.n = 0


class Stream:
    def __init__(self, name):
        self.name = name
        self.sem = None
        self.ops = []
        self.count = 0
        self.waited = {}
        self.serial = False

    def wait(self, *toks):
        for t in toks:
            if t is None:
                continue
            if isinstance(t, Res):
                self.wait(*t.t.values())
                continue
            if isinstance(t, (list, tuple)):
                self.wait(*t)
                continue
            k = id(t.sem)
            if self.waited.get(k, 0) >= t.n:
                continue
            self.waited[k] = t.n
            self.ops.append((0, t.sem, t.n))

    def call(self, meth, *a, mark=True, **kw):
        fn = (lambda e, meth=meth, a=a, kw=kw: getattr(e, meth)(*a, **kw))
        if self.serial:
            mark = True
            if self.count > 0:
                self.wait(Tok(self.sem, self.count))
        if mark:
            self.count += 1
            self.ops.append((1, fn, (self.sem, 1)))
            return Tok(self.sem, self.count)
        self.ops.append((1, fn, None))
        return None

    def dma(self, dsem, out, in_):
        dsem.n += 16
        fn = (lambda e, o=out, i=in_: e.dma_start(out=o, in_=i))
        self.ops.append((1, fn, (dsem.sem, 16)))
        return Tok(dsem.sem, dsem.n)

    def replay(self, e):
        for o in self.ops:
            if o[0] == 0:
                e.wait_ge(o[1], o[2])
            else:
                ins = o[1](e)
                if o[2] is not None:
                    ins.then_inc(o[2][0], o[2][1])


class Bank:
    def __init__(self, ap):
        self.ap = ap
        self.bf = ap.bitcast(BF16)
        self.free = []


def build_nc():
    nc = bass.Bass("TRN2", target_bir_lowering=False)
    es = ExitStack()

    def din(name, shape, dt=F32):
        return nc.dram_tensor(name, shape, dt, kind="ExternalInput").ap()

    def dsc(name, shape, dt, dbg=False):
        if dbg and DBG:
            return nc.dram_tensor(name, shape, dt, kind="ExternalOutput").ap()
        return nc.dram_tensor(name, shape, dt).ap()

    x_in = din("x", [NTOK, D])
    p_in = din("p", [NTOK, 256])
    w_f = {
        "w1a": din("w1a", [D, DFF]), "w3a": din("w3a", [D, DFF]), "w2a": din("w2a", [DFF, D]),
        "win": din("win", [D, 3584]), "wout": din("wout", [D, D]),
        "w1b": din("w1b", [D, DFF]), "w3b": din("w3b", [D, DFF]), "w2b": din("w2b", [DFF, D]),
        "wg": din("wg", [D, D]), "wpp": din("wpp", [256, D]),
    }
    lwa_in = din("lwa", [2048, 128])
    lwi_in = din("lwi", [2048, 128])
    smalls_in = din("smalls", [128, NS])
    rows_in = din("rows", [128, 2304])
    cs_in = din("cs", [NTOK, 128])
    ident_in = din("ident", [128, 128])
    out = nc.dram_tensor("out", [NTOK, D], F32, kind="ExternalOutput").ap()

    wup_a = dsc("wup_a", [88, 128, 2048], BF16)
    wdn_a = dsc("wdn_a", [44, 128, 2048], BF16)
    win_b = dsc("win_b", [28, 128, 2048], BF16)
    wout_b = dsc("wout_b", [16, 128, 2048], BF16)
    wup_b = dsc("wup_b", [88, 128, 2048], BF16)
    wdn_b = dsc("wdn_b", [44, 128, 2048], BF16)
    wg_b = dsc("wg_b", [16, 128, 2048], BF16)
    wpp_b = dsc("wpp_b", [4, 128, 1024], BF16)
    lw_b = dsc("lw_b", [2, 2048, 128], BF16)

    x1_scr = dsc("x1_scr", [NTOK, D], F32, dbg=True)
    qT_scr = dsc("qT_scr", [2, 32, 128, 512], BF16, dbg=True)
    kv_loc = dsc("kv_loc", [512, 4096], BF16)
    kv_g = dsc("kv_g", [2048, 4096], BF16)
    uT_scr = dsc("uT_scr", [8, 128, NTOK], F32, dbg=True)
    ygT_scr = dsc("ygT_scr", [8, 128, NTOK], F32, dbg=True)
    halo_loc2 = dsc("halo_loc", [16, 512], F32)
    halo_g2 = dsc("halo_g", [64, 512], F32)
    sum_loc2 = dsc("sum_loc", [32, 512], F32)
    sum_g2 = dsc("sum_g", [128, 512], F32)
    halo_loc = halo_loc2.rearrange("a (b c) -> (a b) c", c=8)
    halo_g = halo_g2.rearrange("a (b c) -> (a b) c", c=8)
    sum_loc = sum_loc2.rearrange("a (b c) -> (a b) c", c=8)
    sum_g = sum_g2.rearrange("a (b c) -> (a b) c", c=8)
    mixT_scr = dsc("mixT_scr", [16, 128, NTOK], BF16, dbg=True)
    if DBG:
        dbg_kv = nc.dram_tensor("dbg_kv", [512, 4096], BF16, kind="ExternalOutput").ap()

    SP, ACT, DVE, POOL, PE = (Stream(n) for n in ("sp", "act", "dve", "pool", "pe"))
    for s in (ACT, DVE, POOL, PE):
        s.sem = es.enter_context(nc.semaphore("m_" + s.name))
    for s in (ACT, DVE, POOL):
        s.serial = True

    dsem_list = []

    def dsem(name):
        d_ = DSem(es.enter_context(nc.semaphore(name)))
        dsem_list.append(d_)
        return d_

    def sb(name, shape, dt):
        return es.enter_context(nc.sbuf_tensor("sb_" + name, shape, dt))

    ARENA_KB = 194
    arena = es.enter_context(nc.sbuf_tensor("arena", [128, ARENA_KB * 256], F32))
    aoff = [0]

    def carve(name, shape, dt):
        n = 1
        for d_ in shape[1:]:
            n *= d_
        nb_ = n * (2 if dt == BF16 else 4)
        nb_ = (nb_ + 31) // 32 * 32
        o = aoff[0]
        aoff[0] += nb_
        assert aoff[0] <= ARENA_KB * 1024, (name, aoff[0])
        ap = arena[:, o // 4:(o + nb_) // 4]
        if dt == BF16:
            ap = ap.bitcast(BF16)
        ap = ap[:, 0:n]
        if len(shape) == 3:
            ap = ap.rearrange("p (a b) -> p a b", a=shape[1])
        elif len(shape) == 4:
            ap = ap.rearrange("p (a b c) -> p a b c", a=shape[1], b=shape[2])
        elif len(shape) == 5:
            ap = ap.rearrange("p (a b c d) -> p a b c d", a=shape[1], b=shape[2], c=shape[3])
        return ap

    smalls = sb("smalls", [128, NS], F32)
    rows = sb("rows", [128, 2304], F32)
    ident_f = sb("ident_f", [128, 128], F32)
    ident_b = sb("ident_b", [128, 128], BF16)
    ones_b = sb("ones_b", [128, 128], BF16)
    neghalf = sb("neghalf", [128, 8], F32)
    cdec = sb("cdec", [128, 16], F32)
    tsm = sb("tsm", [128, 8, 16], F32)
    s_const = dsem("d_const")

    dbk = [es.enter_context(nc.psum_tensor(f"dbk{i}", [128, 1024], F32)) for i in range(4)]
    banks = [Bank(dbk[j // 2][:, (j % 2) * 512:(j % 2 + 1) * 512]) for j in range(8)]
    nbc = [0]

    def nb():
        b = banks[nbc[0] % 8]
        nbc[0] += 1
        return b

    def pe_take(b):
        PE.wait(b.free)
        b.free = []

    t_c = [SP.dma(s_const, smalls[:], smalls_in), SP.dma(s_const, rows[:], rows_in),
           SP.dma(s_const, ident_f[:], ident_in)]
    c_tok = t_c[-1]
    DVE.wait(c_tok)
    DVE.call("tensor_copy", out=ident_b[:], in_=ident_f[:])
    DVE.call("memset", ones_b[:], 1.0)
    k0 = DVE.call("memset", neghalf[:], -0.5)
    lam = smalls[:, C_LAM:C_LAM + 16]
    T = lambda i: tsm[:, i, :]
    DVE.wait(k0)
    k1 = DVE.call("tensor_scalar", out=T(7), in0=lam, scalar1=-1.0, scalar2=None, op0=ALU.mult)
    DVE.wait(k1)
    k1 = DVE.call("tensor_tensor", out=T(0), in0=lam, in1=T(7), op=ALU.max)
    ACT.wait(k1)
    k2 = ACT.call("activation", out=T(1), in_=T(0), func=AF.Exp, scale=-1.0)
    ACT.wait(k2)
    k3 = ACT.call("activation", out=T(2), in_=T(1), func=AF.Ln, bias=1.0)
    DVE.wait(k3)
    DVE.call("tensor_scalar", out=T(3), in0=T(1), scalar1=-0.25, scalar2=1.0 / 3.0, op0=ALU.mult, op1=ALU.add)
    k4 = DVE.call("tensor_tensor", out=T(3), in0=T(3), in1=T(1), op=ALU.mult)
    DVE.wait(k4)
    k4 = DVE.call("tensor_scalar", out=T(3), in0=T(3), scalar1=-0.5, scalar2=None, op0=ALU.add)
    DVE.wait(k4)
    k4 = DVE.call("tensor_tensor", out=T(3), in0=T(3), in1=T(1), op=ALU.mult)
    DVE.wait(k4)
    k4 = DVE.call("tensor_scalar", out=T(3), in0=T(3), scalar1=1.0, scalar2=None, op0=ALU.add)
    DVE.wait(k4)
    k4 = DVE.call("tensor_tensor", out=T(3), in0=T(3), in1=T(1), op=ALU.mult)
    DVE.wait(k4)
    k5 = DVE.call("tensor_scalar", out=T(4), in0=T(1), scalar1=-100.0, scalar2=6.0, op0=ALU.mult, op1=ALU.add)
    DVE.wait(k5)
    k5 = DVE.call("tensor_scalar", out=T(4), in0=T(4), scalar1=0.0, scalar2=1.0, op0=ALU.max, op1=ALU.min)
    DVE.wait(k5)
    k6 = DVE.call("tensor_tensor", out=T(5), in0=T(3), in1=T(2), op=ALU.subtract)
    DVE.wait(k6)
    k6 = DVE.call("tensor_tensor", out=T(5), in0=T(5), in1=T(4), op=ALU.mult)
    DVE.wait(k6)
    k6 = DVE.call("tensor_tensor", out=T(5), in0=T(5), in1=T(2), op=ALU.add)
    DVE.wait(k6)
    k7 = DVE.call("tensor_scalar", out=T(6), in0=lam, scalar1=-1.0, scalar2=0.0, op0=ALU.mult, op1=ALU.max)
    DVE.wait(k7)
    k7 = DVE.call("tensor_tensor", out=T(6), in0=T(6), in1=T(5), op=ALU.add)
    DVE.wait(k7)
    k_cdec = DVE.call("tensor_scalar", out=cdec[:], in0=T(6), scalar1=-8.0, scalar2=None, op0=ALU.mult)
    const_res = Res()
    const_res.add(k_cdec)
    const_res.add(c_tok)

    cast_sems = {}
    cast_tok = {}

    def cast_dma(key, out_ap, in_ap):
        if key not in cast_sems:
            cast_sems[key] = dsem("c_" + key)
        cast_tok[key] = POOL.dma(cast_sems[key], out_ap, in_ap)

    def cast_up(key, dst, w1, w3):
        for cg in range(22):
            for kq in range(4):
                ov = dst[cg * 4 + kq].rearrange("p (k w c) -> p k w c", k=4, w=2)
                for wi, w in enumerate((w1, w3)):
                    iv = w[kq * 512:(kq + 1) * 512, cg * 256:(cg + 1) * 256].rearrange("(k p) c -> p k c", p=128)
                    cast_dma(key, ov[:, :, wi, :], iv)

    def cast_T(key, dst, w, n_cg, n_kq, ku=4):
        for cg in range(n_cg):
            for kq in range(n_kq):
                ov = dst[cg * n_kq + kq].rearrange("p (k c) -> p k c", k=ku)
                iv = w[kq * ku * 128:(kq + 1) * ku * 128, cg * 512:(cg + 1) * 512].rearrange("(k p) c -> p k c", p=128)
                cast_dma(key, ov, iv)

    late_casts = []

    xt = carve("xt", [128, 4, D], F32)
    xb = carve("xb", [128, 4, D], BF16)
    hT = [carve("hT0", [128, 16, TT], BF16), carve("hT1", [128, 16, TT], BF16)]
    aT = carve("aT", [128, 44, TT], BF16)
    NR = 8
    ring = [carve(f"ring{i}", [128, 2048], BF16) for i in range(NR)]
    ring_sem = [dsem(f"d_ring{i}") for i in range(NR)]
    ring_free = [None] * NR
    ucount = [0]
    silu_tmp = [carve("silu0", [128, TT], F32), carve("silu1", [128, TT], F32)]
    silu_free = [None, None]
    silu_i = [0]
    ss = sb("ss", [128, 8], F32)
    ms = sb("ms", [128, 8], F32)
    rstd = sb("rstd", [128, 8], F32)
    s_x = dsem("d_x")
    s_st = dsem("d_store")
    all_store_sems = [s_st]

    def stsem(name):
        d = dsem(name)
        all_store_sems.append(d)
        return d
    xt_res, xb_res, junk_res, st_res = Res(), Res(), Res(), Res()
    hT_w = [Res(), Res()]
    hT_r = [Res(), Res()]
    aT_r = Res()
    aT_w = Res()

    def barrier():
        toks = [Tok(S.sem, S.count) for S in (ACT, DVE, POOL, PE) if S.count > 0]
        for S in (SP, ACT, DVE, POOL, PE):
            S.wait(toks)
            S.wait([Tok(d.sem, d.n) for d in dsem_list if d.n > 0])
            if cc_state[0] > 0:
                S.wait(Tok(s_cc, cc_state[0]))

    cc_state = [0]
    s_cc = es.enter_context(nc.semaphore("cc"))

    def load_unit(src, n, ctok):
        s = ucount[0] % NR
        ucount[0] += 1
        SP.wait(ring_free[s], ctok)
        tl = SP.dma(ring_sem[s], ring[s][:, 0:n], src)
        return s, tl

    def proj_T(act, act_ready, n_kc, src, n_cg, ctok, epilogue, ku=4, cg_list=None):
        nkq = n_kc // ku
        last_tok = None
        for cg in (cg_list if cg_list is not None else range(n_cg)):
            bk = [nb() for _ in range(4)]
            for b in bk:
                pe_take(b)
            PE.wait(act_ready)
            for kq in range(nkq):
                s, tl = load_unit(src[cg * nkq + kq], ku * 512, ctok)
                PE.wait(tl)
                v = ring[s][:, 0:ku * 512].rearrange("p (k c) -> p k c", k=ku)
                for kcl in range(ku):
                    kc = kq * ku + kcl
                    for ts in range(4):
                        fin = (kcl == ku - 1 and ts == 3)
                        tok = PE.call("matmul", bk[ts].ap, act(kc, ts), v[:, kcl, :],
                                      start=(kc == 0), stop=(kc == n_kc - 1), mark=fin)
                        if fin:
                            ring_free[s] = tok
                            last_tok = tok
            epilogue(cg, bk, last_tok)
        return last_tok

    def proj_F(act, act_ready, n_kc, src, cg_list, ctok, epilogue, dual):
        nkq = n_kc // 4
        last_tok = None
        for ci, cg in enumerate(cg_list):
            bk = [nb() for _ in range(4)]
            for b in bk:
                pe_take(b)
            PE.wait(act_ready)
            for kq in range(nkq):
                s, tl = load_unit(src[cg * nkq + kq], 2048, ctok)
                PE.wait(tl)
                if dual:
                    v = ring[s][:].rearrange("p (k w c) -> p k w c", k=4, w=2)
                else:
                    v = ring[s][:].rearrange("p (k c) -> p k c", k=4)
                for kcl in range(4):
                    kc = kq * 4 + kcl
                    for j in range(4):
                        if dual:
                            lhsT = v[:, kcl, j // 2, (j % 2) * 128:(j % 2 + 1) * 128]
                        else:
                            lhsT = v[:, kcl, j * 128:(j + 1) * 128]
                        fin = (kcl == 3 and j == 3)
                        tok = PE.call("matmul", bk[j].ap, lhsT, act(kc),
                                      start=(kc == 0), stop=(kc == n_kc - 1), mark=fin)
                        if fin:
                            ring_free[s] = tok
                            last_tok = tok
            epilogue(cg, bk, last_tok)
        return last_tok

    def norm_front(nsub=4):
        ta = None
        for ts in range(nsub):
            ACT.wait(xt_res, xb_res)
            ta = ACT.call("activation", out=xb[:, ts, :], in_=xt[:, ts, :], func=AF.Square, accum_out=ss[:, ts:ts + 1])
            xb_res.add(ta)
        xt_res.add(ta)
        DVE.wait(ta)
        td = DVE.call("tensor_scalar", out=ms[:, 0:4], in0=ss[:, 0:4], scalar1=1.0 / D, scalar2=EPS,
                      op0=ALU.mult, op1=ALU.add)
        ACT.wait(td)
        tq = ACT.call("activation", out=ms[:, 4:8], in_=ms[:, 0:4], func=AF.Sqrt)
        DVE.wait(tq)
        tp = DVE.call("reciprocal", out=rstd[:, 0:4], in_=ms[:, 4:8])
        return tp

    def norm_scale(tp):
        for ts in range(4):
            DVE.wait(tp, xb_res, xt_res)
            td = DVE.call("tensor_scalar", out=xb[:, ts, :], in0=xt[:, ts, :], scalar1=rstd[:, ts:ts + 1],
                          scalar2=None, op0=ALU.mult)
            xb_res.add(td)
            xt_res.add(td)

    def norm_back(b, gcol):
        hT_w[b] = Res()
        for kc in range(16):
            bk = nb()
            pe_take(bk)
            PE.wait(xb_res)
            tok = None
            for ts in range(4):
                tok = PE.call("transpose", out=bk.bf[:, ts * 128:(ts + 1) * 128],
                              in_=xb[:, ts, kc * 128:(kc + 1) * 128], identity=ident_b[:], mark=(ts == 3))
            xb_res.add(tok)
            DVE.wait(tok, hT_r[b])
            te = DVE.call("tensor_scalar", out=hT[b][:, kc, :], in0=bk.bf[:, 0:TT],
                          scalar1=smalls[:, gcol + kc:gcol + kc + 1], scalar2=None, op0=ALU.mult)
            bk.free = [te]
            hT_w[b].add(te)
        hT_r[b] = Res()

    def ffn(b, up_src, dn_src, ctok_up, ctok_dn):
        def up_epi(cg, bk, gtok):
            for mi in range(2):
                m = cg * 2 + mi
                A, Bk = bk[mi], bk[2 + mi]
                si = silu_i[0] % 2
                silu_i[0] += 1
                ACT.wait(gtok, silu_free[si])
                ta = ACT.call("activation", out=silu_tmp[si][:], in_=A.ap, func=AF.Silu)
                A.free = [ta]
                DVE.wait(ta, gtok, aT_r)
                td = DVE.call("tensor_tensor", out=aT[:, m, :], in0=silu_tmp[si][:], in1=Bk.ap, op=ALU.mult)
                Bk.free = [td]
                silu_free[si] = td
                aT_w.add(td)

        lt = proj_F(lambda kc: hT[b][:, kc, :], hT_w[b], 16, up_src, list(range(22)), ctok_up, up_epi, True)
        hT_r[b].add(lt)

        def dn_epi(fg, bk, gtok):
            for ts in range(4):
                DVE.wait(gtok, xt_res)
                sl = xt[:, ts, fg * 512:(fg + 1) * 512]
                td = DVE.call("scalar_tensor_tensor", out=sl, in0=bk[ts].ap, scalar=0.5, in1=sl,
                              op0=ALU.mult, op1=ALU.add)
                bk[ts].free = [td]
                xt_new.add(td)

        xt_new = Res()
        lt = proj_T(lambda kc, ts: aT[:, kc, ts * 128:(ts + 1) * 128], aT_w, 44, dn_src, 4, ctok_dn, dn_epi)
        aT_r.set(lt)
        for t in xt_new.t.values():
            xt_res.add(t)

    qf = carve("qf", [128, 4, 128], F32)
    qr4 = [carve(f"qr{i}", [128, 4, 128], BF16) for i in range(4)]
    rt = [carve(f"rt{i}", [128, 4, 64], F32) for i in range(4)]
    ssq = sb("ssq", [128, 8], F32)
    msq = sb("msq", [128, 8], F32)
    rsq = sb("rsq", [128, 8], F32)
    junk2 = carve("junk2", [128, 128], F32)
    cs_t = carve("cs_t", [128, 4, 128], F32)
    qT_st = [carve("qTst0", [128, 512], BF16), carve("qTst1", [128, 512], BF16)]
    vb = carve("vb", [128, 256], BF16)
    ust = [carve("ust0", [128, TT], F32), carve("ust1", [128, TT], F32)]
    yt = [carve(f"yt{i}", [128, TT], F32) for i in range(3)]
    s_cs = dsem("d_cs")
    qst_sem = [stsem("d_qst0"), stsem("d_qst1")]
    vb_sem = stsem("d_vb")
    ust_sem = [stsem("d_ust0"), stsem("d_ust1")]
    yt_sem = stsem("d_yt")
    qf_res, cs_res, vb_res = Res(), Res(), Res()
    qr_res4 = [Res() for _ in range(4)]
    qTst_res = [Res(), Res()]
    ust_res = [Res(), Res()]
    yt_res = Res()
    qst_i = [0]
    ust_i = [0]
    pend_tr = []

    aoff_AE = aoff[0]
    kvl_k = kv_loc[0:256, :]
    kvl_v = kv_loc[256:512, :].rearrange("(g p a) (b d) -> g p (a b) d", g=2, p=128, d=128)

    def qk_epilogue(i, cg, ts, bank, gtok):
        nh = 4 if cg < 2 else 2
        gain = rows[:, 2048:2176] if cg < 2 else rows[:, 2176:2304]
        W = nh * 128
        ACT.wait(gtok, qf_res)
        ta = ACT.call("activation", out=qf[:, 0:nh, :], in_=bank.ap[:, 0:W].rearrange("p (h d) -> p h d", h=nh),
                      func=AF.Copy)
        if cg == 2:
            ACT.wait(vb_res)
            tv = ACT.call("activation", out=vb[:], in_=bank.ap[:, 256:512], func=AF.Copy)
            vb_res.set(tv)
            bank.free = [tv]
            ch = i * 4 + ts
            POOL.wait(tv)
            tvs = POOL.dma(vb_sem, kvl_v[:, :, ch, :].rearrange("g p d -> p g d"),
                           vb[:].rearrange("p (g d) -> p g d", g=2))
            vb_res.add(tvs)
        else:
            bank.free = [ta]
        for hh in range(nh):
            ACT.wait(ta)
            ta2 = ACT.call("activation", out=junk2[:], in_=qf[:, hh, :], func=AF.Square,
                           accum_out=ssq[:, hh:hh + 1])
        DVE.wait(ta2)
        td = DVE.call("tensor_scalar", out=msq[:, 0:nh], in0=ssq[:, 0:nh], scalar1=1.0 / 128, scalar2=EPS,
                      op0=ALU.mult, op1=ALU.add)
        ACT.wait(td)
        tq = ACT.call("activation", out=msq[:, 4:4 + nh], in_=msq[:, 0:nh], func=AF.Sqrt)
        DVE.wait(tq)
        tp = DVE.call("reciprocal", out=rsq[:, 0:nh], in_=msq[:, 4:4 + nh])
        DVE.wait(tp)
        td = DVE.call("tensor_tensor", out=qf[:, 0:nh, :], in0=qf[:, 0:nh, :],
                      in1=rsq[:, 0:nh].unsqueeze(2).to_broadcast([128, nh, 128]), op=ALU.mult)
        DVE.wait(td)
        td = DVE.call("tensor_tensor", out=qf[:, 0:nh, :], in0=qf[:, 0:nh, :],
                      in1=gain.unsqueeze(1).to_broadcast([128, nh, 128]), op=ALU.mult)
        qr = qr4[ts]
        DVE.wait(td, cs_res, qr_res4[ts])
        x1 = qf[:, 0:nh, 0::2]
        x2 = qf[:, 0:nh, 1::2]
        cc = cs_t[:, ts, 0:64].unsqueeze(1).to_broadcast([128, nh, 64])
        sn = cs_t[:, ts, 64:128].unsqueeze(1).to_broadcast([128, nh, 64])
        DVE.call("tensor_tensor", out=rt[0][:, 0:nh, :], in0=x1, in1=cc, op=ALU.mult, mark=False)
        DVE.call("tensor_tensor", out=rt[1][:, 0:nh, :], in0=x2, in1=sn, op=ALU.mult, mark=False)
        DVE.call("tensor_tensor", out=rt[2][:, 0:nh, :], in0=x1, in1=sn, op=ALU.mult, mark=False)
        td = DVE.call("tensor_tensor", out=rt[3][:, 0:nh, :], in0=x2, in1=cc, op=ALU.mult)
        DVE.wait(td)
        DVE.call("tensor_tensor", out=qr[:, 0:nh, 0::2], in0=rt[0][:, 0:nh, :], in1=rt[1][:, 0:nh, :],
                 op=ALU.subtract, mark=False)
        td = DVE.call("tensor_tensor", out=qr[:, 0:nh, 1::2], in0=rt[2][:, 0:nh, :], in1=rt[3][:, 0:nh, :],
                      op=ALU.add)
        qf_res.set(td)
        cs_res.add(td)

        def do_tr(td=td, qr=qr, nh=nh, W=W, cg=cg, ts=ts, i=i):
            bk = nb()
            pe_take(bk)
            PE.wait(td)
            tok = None
            for hh in range(nh):
                tok = PE.call("transpose", out=bk.bf[:, hh * 128:(hh + 1) * 128], in_=qr[:, hh, :],
                              identity=ident_b[:], mark=(hh == nh - 1))
            qr_res4[ts].set(tok)
            si = qst_i[0] % 2
            qst_i[0] += 1
            ACT.wait(tok, qTst_res[si])
            te = ACT.call("activation", out=qT_st[si][:, 0:W], in_=bk.bf[:, 0:W], func=AF.Copy)
            bk.free = [te]
            POOL.wait(te)
            if cg < 2:
                tst = POOL.dma(qst_sem[si], qT_scr[cg, i * 4 + ts], qT_st[si][:])
            else:
                tst = POOL.dma(qst_sem[si],
                               kvl_k.rearrange("(g d) t -> d g t", g=2)[:, :, i * TT + ts * 128:i * TT + (ts + 1) * 128],
                               qT_st[si][:, 0:256].rearrange("p (g t) -> p g t", g=2))
            qTst_res[si].set(te)
            qTst_res[si].add(tst)

        pend_tr.append(do_tr)

    def flush_tr():
        while pend_tr:
            pend_tr.pop(0)()

    halo_v = halo_loc.rearrange("(j p) c -> j p c", p=128)

    def uy_epilogue(i, cg, bk, gtok):
        flush_tr()
        for mi in range(4):
            if cg < 5:
                j = (cg - 3) * 4 + mi
                si = ust_i[0] % 2
                ust_i[0] += 1
                ACT.wait(gtok, ust_res[si])
                ta = ACT.call("activation", out=ust[si][:], in_=bk[mi].ap, func=AF.Copy)
                bk[mi].free = [ta]
                POOL.wait(ta)
                tst = POOL.dma(ust_sem[si], uT_scr[j, :, i * TT:(i + 1) * TT], ust[si][:])
                ust_res[si].set(ta)
                ust_res[si].add(tst)
                if i == 0:
                    ust_res[si].add(POOL.dma(ust_sem[si], halo_v[j, :, 0:1], ust[si][:, 0:1]))
                if i == NTILE - 1:
                    ust_res[si].add(POOL.dma(ust_sem[si], halo_v[j, :, 1:3], ust[si][:, TT - 2:TT]))
            else:
                j = (cg - 5) * 4 + mi
                y0, y1, y2 = yt[0], yt[1], yt[2]
                ACT.wait(gtok, yt_res)
                ta = ACT.call("activation", out=y0[:], in_=bk[mi].ap, func=AF.Copy)
                bk[mi].free = [ta]
                DVE.wait(ta, yt_res)
                td = DVE.call("tensor_tensor", out=y1[:], in0=y0[:], in1=y0[:], op=ALU.mult)
                DVE.wait(td)
                td = DVE.call("tensor_scalar", out=y1[:], in0=y1[:], scalar1=0.044715, scalar2=1.0,
                              op0=ALU.mult, op1=ALU.add)
                DVE.wait(td)
                td = DVE.call("tensor_tensor", out=y1[:], in0=y1[:], in1=y0[:], op=ALU.mult)
                ACT.wait(td)
                ta = ACT.call("activation", out=y1[:], in_=y1[:], func=AF.Sigmoid, scale=GELU_K)
                DVE.wait(ta)
                td = DVE.call("tensor_tensor", out=y2[:], in0=y1[:], in1=y0[:], op=ALU.mult)
                POOL.wait(td)
                tst = POOL.dma(yt_sem, ygT_scr[j, :, i * TT:(i + 1) * TT], y2[:])
                yt_res.set(td)
                yt_res.add(tst)

    pipe2_units = []
    pipe2_cls = []
    try:
        x_tiles = x_in.rearrange("(i ts p) f -> i p ts f", ts=4, p=128)
        x1_tiles = x1_scr.rearrange("(i ts p) f -> i p ts f", ts=4, p=128)
        cs_tiles = cs_in.rearrange("(i ts p) c -> i p ts c", ts=4, p=128)

        save_off = aoff[0]
        aoff[0] = 0
        NSTG = 8
        stg_f = [carve(f"stgf{i}", [128, 2048], F32) for i in range(NSTG)]
        stg_b = [carve(f"stgb{i}", [128, 2048], BF16) for i in range(NSTG)]
        aoff[0] = save_off
        pre_units = []

        def add_up(dst, w1, w3):
            for cg in range(22):
                for kq in range(4):
                    lds = []
                    for wi, w in enumerate((w1, w3)):
                        src = w[kq * 512:(kq + 1) * 512, cg * 256:(cg + 1) * 256].rearrange("(k p) c -> p k c", p=128)
                        lds.append((lambda st, wi=wi: st.rearrange("p (k w c) -> p k w c", k=4, w=2)[:, :, wi, :], src))
                    pre_units.append((dst[cg * 4 + kq], lds, 2048))

        def add_T(dst, w, n_cg, n_kq, ku=4):
            for cg in range(n_cg):
                for kq in range(n_kq):
                    src = w[kq * ku * 128:(kq + 1) * ku * 128, cg * 512:(cg + 1) * 512].rearrange("(k p) c -> p k c", p=128)
                    lds = [(lambda st, ku=ku: st[:, 0:ku * 512].rearrange("p (k c) -> p k c", k=ku), src)]
                    pre_units.append((dst[cg * n_kq + kq][:, 0:ku * 512], lds, ku * 512))

        add_up(wup_a, w_f["w1a"], w_f["w3a"])
        add_T(wdn_a, w_f["w2a"], 4, 11)
        add_T(win_b, w_f["win"], 7, 4)
        for x_, src_ in enumerate((lwa_in, lwi_in)):
            pre_units.append((lw_b[x_].rearrange("(a p) o -> p a o", p=128),
                              [(lambda st: st.rearrange("p (a o) -> p a o", a=16), src_.rearrange("(a p) o -> p a o", p=128))],
                              2048))
        add_T(wout_b, w_f["wout"], 4, 4)
        add_up(wup_b, w_f["w1b"], w_f["w3b"])
        add_T(wdn_b, w_f["w2b"], 4, 11)
        add_T(wg_b, w_f["wg"], 4, 4)
        add_T(wpp_b, w_f["wpp"], 4, 1, ku=2)
        class PrePipe:
            def __init__(self, units_, stg_f_, stg_b_, tag, use_act):
                self.units = units_
                self.sf = stg_f_
                self.sb_ = stg_b_
                self.ns = len(stg_f_)
                self.depth = self.ns - 1
                self.ld_sem = [dsem(f"d_pl{tag}{i}") for i in range(self.ns)]
                self.st_sem = [dsem(f"d_ps{tag}{i}") for i in range(self.ns)]
                self.ld_tok = {}
                self.cast_rd = [None] * self.ns
                self.pst_tok = [None] * self.ns
                self.use_act = use_act
                self.u = 0
                self.total = len(units_) + self.depth

            def step(self):
                u = self.u
                if u >= self.total:
                    return False
                self.u += 1
                NU_ = len(self.units)
                if u < NU_:
                    sl = u % self.ns
                    SP.wait(self.cast_rd[sl])
                    for vf, src in self.units[u][1]:
                        self.ld_tok[u] = SP.dma(self.ld_sem[sl], vf(self.sf[sl]), src)
                v_ = u - self.depth
                if v_ >= 0:
                    sl = v_ % self.ns
                    dst, _, n_ = self.units[v_]
                    if v_ % 2 == 0 or not self.use_act:
                        DVE.wait(self.ld_tok.pop(v_), self.pst_tok[sl])
                        tk = DVE.call("tensor_copy", out=self.sb_[sl][:, 0:n_], in_=self.sf[sl][:, 0:n_])
                    else:
                        ACT.wait(self.ld_tok.pop(v_), self.pst_tok[sl])
                        tk = ACT.call("activation", out=self.sb_[sl][:, 0:n_], in_=self.sf[sl][:, 0:n_], func=AF.Copy)
                    self.cast_rd[sl] = tk
                    SP.wait(tk)
                    if len(dst.shape) == 3:
                        self.pst_tok[sl] = SP.dma(self.st_sem[sl], dst,
                                                  self.sb_[sl][:, 0:n_].rearrange("p (a o) -> p a o", a=dst.shape[1]))
                    else:
                        self.pst_tok[sl] = SP.dma(self.st_sem[sl], dst, self.sb_[sl][:, 0:n_])
                return True

            def drain(self):
                while self.step():
                    pass

        N_EARLY = 88 + 44 + 28 + 2
        pipe1 = PrePipe(pre_units[:N_EARLY], stg_f, stg_b, "a", True)
        pipe1.drain()
        pipe2_units.extend(pre_units[N_EARLY:])
        pipe2_cls.append(PrePipe)
        for k_ in ("upa", "dna", "win", "lw", "wout", "upb", "dnb", "wg", "wpp"):
            cast_tok[k_] = None
        late = []
        barrier()
        tx = POOL.dma(s_x, xt[:], x_tiles[0])
        xt_res.set(tx)

        DVE.wait(const_res)
        ACT.wait(const_res)
        PE.wait(const_res)
        tp = norm_front()
        norm_scale(tp)
        cp(1)
        for i in range(NTILE):
            norm_back(0, C_G1)
            cp(2)
            ffn(0, wup_a, wdn_a, cast_tok["upa"], cast_tok["dna"])
            cp(3)
            POOL.wait(xt_res)
            t_st = POOL.dma(s_st, x1_tiles[i], xt[:])
            tp = norm_front()
            norm_scale(tp)
            norm_back(1, C_GM)
            cp(4)
            SP.wait(cs_res)
            tcs = SP.dma(s_cs, cs_t[:], cs_tiles[i])
            cs_res.set(tcs)

            def qkv_epi(cg, bk, gtok, i=i):
                flush_tr()
                for ts in range(4):
                    qk_epilogue(i, cg, ts, bk[ts], gtok)

            lt = proj_T(lambda kc, ts: hT[1][:, kc, ts * 128:(ts + 1) * 128], hT_w[1], 16, win_b, 7,
                        cast_tok["win"], qkv_epi, cg_list=[0, 1, 2])
            hT_r[1].add(lt)
            cp(5)
            if i + 1 < NTILE:
                POOL.wait(t_st, xt_res)
                tx = POOL.dma(s_x, xt[:], x_tiles[i + 1])
                xt_res.set(tx)
                tp = norm_front()
                norm_scale(tp)
            lt = proj_F(lambda kc: hT[1][:, kc, :], hT_w[1], 16, win_b, [3, 4, 5, 6], cast_tok["win"],
                        lambda cg, bk, gtok, i=i: uy_epilogue(i, cg, bk, gtok), False)
            hT_r[1].add(lt)
            n_late = (len(late) + NTILE - 1) // NTILE
            for _ in range(n_late):
                if late:
                    late.pop(0)()
        while late:
            late.pop(0)()


    except StopEmit:
        pass

    groups = [[0, 1, 2, 3], [4, 5, 6, 7]]
    POOL.wait([Tok(d.sem, d.n) for d in all_store_sems if d.n > 0])
    if not NO_CC:
        for c_ in range(8):
            POOL.ops.append((1, lambda e, c_=c_: e.collective_compute(
                "AllGather", ALU.bypass, replica_groups=groups, ins=[kv_loc[c_ * 64:(c_ + 1) * 64, :]],
                outs=[kv_g[c_ * 256:(c_ + 1) * 256, :]]), (s_cc, 1)))
        POOL.ops.append((1, lambda e: e.collective_compute("AllGather", ALU.bypass, replica_groups=groups,
                                                           ins=[halo_loc2], outs=[halo_g2]), (s_cc, 1)))
        cc_tok = Tok(s_cc, 9)
    else:
        cc_tok = None
    if DBG and NTILE == 8:
        POOL.dma(s_st, dbg_kv, kv_loc)


    cc_state[0] = 9 if not NO_CC else 0

    def allgather(src, dst, rows):
        if not NO_CC:
            POOL.ops.append((1, lambda e: e.collective_compute("AllGather", ALU.bypass, replica_groups=groups,
                                                               ins=[src], outs=[dst]), (s_cc, 1)))
            cc_state[0] += 1
        else:
            for r in range(4):
                POOL.dma(s_st, dst[r * rows:(r + 1) * rows, :], src)

    if NO_CC:
        for r in range(4):
            for c_ in range(8):
                POOL.dma(s_st, kv_g[c_ * 256 + r * 64:c_ * 256 + (r + 1) * 64, :], kv_loc[c_ * 64:(c_ + 1) * 64, :])
            POOL.dma(s_st, halo_g[r * 1024:(r + 1) * 1024, :], halo_loc)
    barrier()

    if 'C' in PHASES:
        aoff[0] = 0
        kT_sb = carve("kT_sb", [128, SEQ], BF16)
        v_sb = carve("v_sb", [128, 128, 128], BF16)
        qT_sb = [carve("qTsb0", [128, 512], BF16), carve("qTsb1", [128, 512], BF16)]
        NP = 3
        Pr = [carve(f"P{i}", [128, 1024], BF16) for i in range(NP)]
        Ps = [carve(f"Ps{i}", [128, 512], BF16) for i in range(NP)]
        rl = carve("rl", [128, 512], F32)
        oT = [carve("oT0", [128, 512], BF16), carve("oT1", [128, 512], BF16)]
        s_kv = dsem("d_kv")
        s_q = [dsem("d_q0"), dsem("d_q1")]
        s_o = [stsem("d_o0"), stsem("d_o1")]
        NQT = NTILE * 4
        NPAIR = NKC // 2
        SCALE = 1.0 / math.sqrt(128.0)
        units = [(g, qt, pr) for g in range(2) for qt in range(NQT) for pr in range(NPAIR)]
        q_load_tok = {}
        q_free = [None, None]
        kv_tok = {}
        kv_free = Res()
        S_free = [None, None]
        P_free = [None] * NP
        tokE = {}
        tokD = {}
        oT_res = [Res(), Res()]

        def load_q(g, qt):
            ti = g * NQT + qt
            SP.wait(q_free[ti % 2])
            q_load_tok[(g, qt)] = SP.dma(s_q[ti % 2], qT_sb[ti % 2][:], qT_scr[g, qt])

        def load_kv(g):
            SP.wait(kv_free)
            for r in range(4):
                for hf_ in range(2):
                    c1 = 2 * g + hf_
                    SP.dma(s_kv, kT_sb[hf_ * 64:(hf_ + 1) * 64, r * 4096:(r + 1) * 4096],
                           kv_g[c1 * 256 + r * 64:c1 * 256 + (r + 1) * 64, :])
                    c2 = 4 + 2 * g + hf_
                    vv = kv_g[c2 * 256 + r * 64:c2 * 256 + (r + 1) * 64, :].rearrange("p (b d) -> p b d", d=128)
                    kv_tok[g] = SP.dma(s_kv, v_sb[hf_ * 64:(hf_ + 1) * 64, r * 32:(r + 1) * 32, :], vv)

        def emit_S(n):
            g, qt, pr = units[n]
            ti = g * NQT + qt
            if pr == 0:
                if qt == 0:
                    load_kv(g)
                    load_q(g, 0)
                if qt + 1 < NQT:
                    load_q(g, qt + 1)
                PE.wait(q_load_tok[(g, qt)], kv_tok[g])
            PE.wait(S_free[n % 2])
            tok = None
            for j in range(2):
                kc = 2 * pr + j
                tok = PE.call("matmul", dbk[n % 2][:, j * 512:(j + 1) * 512], kT_sb[:, kc * 128:(kc + 1) * 128],
                              qT_sb[ti % 2][:], start=True, stop=True, mark=(j == 1))
            if pr == NPAIR - 1:
                q_free[ti % 2] = tok
                if qt == NQT - 1:
                    kv_free.add(tok)
            ACT.wait(tok, P_free[n % NP])
            te = ACT.call("activation", out=Pr[n % NP][:], in_=dbk[n % 2][:, 0:1024], func=AF.Exp, scale=SCALE)
            S_free[n % 2] = te
            tokE[n] = te
            DVE.wait(te, P_free[n % NP])
            tokD[n] = DVE.call("tensor_tensor", out=Ps[n % NP][:], in0=Pr[n % NP][:, 0:512], in1=Pr[n % NP][:, 512:1024],
                               op=ALU.add)

        def emit_PV(n):
            g, qt, pr = units[n]
            ti = g * NQT + qt
            a = ti % 2
            O, L = banks[4 + 2 * a], banks[5 + 2 * a]
            if pr == 0:
                pe_take(O)
                pe_take(L)
            PE.wait(tokE.pop(n))
            tok = None
            for j in range(2):
                kc = 2 * pr + j
                first = (pr == 0 and j == 0)
                last = (pr == NPAIR - 1 and j == 1)
                PE.call("matmul", O.ap, v_sb[:, kc, :], Pr[n % NP][:, j * 512:(j + 1) * 512], start=first, stop=last,
                        mark=False)
            PE.wait(tokD.pop(n))
            tok = PE.call("matmul", L.ap, ones_b[:], Ps[n % NP][:], start=(pr == 0), stop=(pr == NPAIR - 1))
            P_free[n % NP] = tok
            if pr == NPAIR - 1:
                if qt == NQT - 1:
                    kv_free.add(tok)
                DVE.wait(tok)
                td = DVE.call("reciprocal", out=rl[:], in_=L.ap)
                L.free = [td]
                DVE.wait(oT_res[a])
                td2 = DVE.call("tensor_tensor", out=oT[a][:], in0=O.ap, in1=rl[:], op=ALU.mult)
                O.free = [td2]
                POOL.wait(td2)
                tst = POOL.dma(s_o[a], mixT_scr[g * 4:(g + 1) * 4, :, qt * 128:(qt + 1) * 128].rearrange("h d t -> d h t"),
                               oT[a][:].rearrange("p (h t) -> p h t", h=4))
                oT_res[a].set(tst)

        for b_ in banks:
            b_.free = []
        ACT.serial = False
        pipe2 = None
        if pipe2_cls:
            stg_f2 = [carve(f"stgf2{i}", [128, 2048], F32) for i in range(4)]
            stg_b2 = [carve(f"stgb2{i}", [128, 2048], BF16) for i in range(4)]
            pipe2 = pipe2_cls[0](pipe2_units, stg_f2, stg_b2, "b", False)
        for n in range(len(units)):
            if n >= 1 and units[n][1] == 0 and units[n][2] == 0:
                emit_PV(n - 1)
                emit_S(n)
            else:
                emit_S(n)
                if n >= 1:
                    emit_PV(n - 1)
            if pipe2 is not None:
                while pipe2.u * len(units) < (n + 1) * pipe2.total:
                    if not pipe2.step():
                        break
        emit_PV(len(units) - 1)
        ACT.serial = True
        if pipe2 is not None:
            pipe2.drain()
        barrier()


    if 'D' in PHASES:
        aoff[0] = 0
        us = carve("us", [128, NTOK + 8], F32)
        uc = carve("uc", [128, NTOK], F32)
        ucb = carve("ucb", [128, NTOK], BF16)
        rb = carve("rb", [128, NTOK], F32)
        ig = carve("ig", [128, NTOK], F32)
        mb = carve("mb", [128, NTOK], F32)
        hf = carve("hf", [128, NTOK], F32)
        lw_sb = carve("lw_sb", [128, 2, 16, 128], BF16)
        hlt = carve("hlt", [128, 4, 8, 8], F32)
        sgt = carve("sgt", [128, 4, 2, 8, 8], F32)
        sumst = carve("sumst", [128, 2, 8, 8], F32)
        hin = carve("hin", [128, 8], F32)
        rsum = carve("rsum", [128, 8], F32)
        s_d = dsem("d_lru")
        s_dg = dsem("d_lrug")
        s_du = dsem("d_lruu")
        s_dy = dsem("d_lruy")
        s_dst = stsem("d_lrust")
        NT8 = NTOK // TT
        hb = us[:, 0:NTOK]

        def dserial(*engs):
            toks = [Tok(S.sem, S.count) for S in engs if S.sem is not None and S.count > 0]
            for S in engs:
                S.wait(toks)

        mask = lambda k, r: smalls[:, C_MASK + k * 4 + r:C_MASK + k * 4 + r + 1]
        t0_ = SP.dma(s_d, lw_sb[:, 0], lw_b[0].rearrange("(a p) o -> p a o", p=128))
        t0_ = SP.dma(s_d, lw_sb[:, 1], lw_b[1].rearrange("(a p) o -> p a o", p=128))
        for r in range(4):
            t0_ = SP.dma(s_d, hlt[:, r], halo_g[r * 1024:(r + 1) * 1024, :].rearrange("(j p) c -> p j c", p=128))
        DVE.wait(t0_)
        DVE.call("memset", sumst[:], 0.0)

        def lru_pass(ps):
            if ps == 1:
                tl = None
                for r in range(4):
                    for d_ in range(2):
                        tl = SP.dma(s_dg, sgt[:, r, d_], sum_g[r * 2048 + d_ * 1024:r * 2048 + (d_ + 1) * 1024, :]
                                    .rearrange("(j p) c -> p j c", p=128))
                DVE.wait(tl)
            for j in range(8):
                dserial(SP, ACT, DVE, POOL, PE)
                SP.wait([Tok(S.sem, S.count) for S in (ACT, DVE, POOL, PE)])
                SP.wait(Tok(s_du.sem, s_du.n))
                tl = SP.dma(s_du, us[:, 2:2 + NTOK], uT_scr[j])
                DVE.wait(tl, t0_)
                PE.wait(t0_)
                DVE.call("tensor_scalar", out=us[:, 0:2], in0=hlt[:, 0, j, 1:3], scalar1=mask(2, 0), scalar2=None, op0=ALU.mult)
                DVE.call("tensor_scalar", out=us[:, 2 + NTOK:3 + NTOK], in0=hlt[:, 0, j, 0:1], scalar1=mask(3, 0), scalar2=None,
                         op0=ALU.mult)
                for r in range(1, 4):
                    DVE.call("scalar_tensor_tensor", out=us[:, 0:2], in0=hlt[:, r, j, 1:3], scalar=mask(2, r), in1=us[:, 0:2],
                             op0=ALU.mult, op1=ALU.add)
                    DVE.call("scalar_tensor_tensor", out=us[:, 2 + NTOK:3 + NTOK], in0=hlt[:, r, j, 0:1], scalar=mask(3, r),
                             in1=us[:, 2 + NTOK:3 + NTOK], op0=ALU.mult, op1=ALU.add)
                cw = lambda tap: smalls[:, C_CW + tap * 8 + j:C_CW + tap * 8 + j + 1]
                DVE.call("tensor_scalar", out=uc[:], in0=us[:, 0:NTOK], scalar1=cw(0), scalar2=smalls[:, C_CB + j:C_CB + j + 1],
                         op0=ALU.mult, op1=ALU.add)
                for tap in range(1, 4):
                    DVE.call("scalar_tensor_tensor", out=uc[:], in0=us[:, tap:tap + NTOK], scalar=cw(tap), in1=uc[:],
                             op0=ALU.mult, op1=ALU.add)
                dserial(ACT, DVE)
                ACT.call("activation", out=ucb[:], in_=uc[:], func=AF.Copy)
                dserial(ACT, PE)
                for d_ in range(2):
                    cd = cdec[:, d_ * 8 + j:d_ * 8 + j + 1]
                    for x_, dst, cb in ((0, rb, C_BA), (1, ig, C_BI)):
                        for tt in range(NT8):
                            bk = nb()
                            pe_take(bk)
                            tok = PE.call("matmul", bk.ap, lw_sb[:, x_, d_ * 8 + j, :], ucb[:, tt * TT:(tt + 1) * TT],
                                          start=True, stop=True)
                            ACT.wait(tok)
                            ta = ACT.call("activation", out=dst[:, tt * TT:(tt + 1) * TT], in_=bk.ap, func=AF.Sigmoid,
                                          bias=smalls[:, cb + d_ * 8 + j:cb + d_ * 8 + j + 1])
                            bk.free = [ta]
                    dserial(ACT, DVE, PE)
                    if ps == 0:
                        DVE.call("reduce_sum", out=rsum[:, 0:1], in_=rb[:], axis=AX.X)
                        dserial(ACT, DVE)
                        ACT.call("activation", out=sumst[:, d_, j, 0:1], in_=rsum[:, 0:1], func=AF.Exp, scale=cd)
                    ACT.call("activation", out=rb[:], in_=rb[:], func=AF.Exp, scale=cd)
                    dserial(ACT, DVE)
                    DVE.call("tensor_tensor", out=mb[:], in0=rb[:], in1=rb[:], op=ALU.mult)
                    DVE.call("tensor_scalar", out=mb[:], in0=mb[:], scalar1=-1.0, scalar2=1.0, op0=ALU.mult, op1=ALU.add)
                    DVE.call("tensor_scalar", out=mb[:], in0=mb[:], scalar1=1e-30, scalar2=None, op0=ALU.max)
                    dserial(ACT, DVE)
                    ACT.call("activation", out=mb[:], in_=mb[:], func=AF.Sqrt)
                    dserial(ACT, DVE)
                    DVE.call("tensor_tensor", out=mb[:], in0=mb[:], in1=ig[:], op=ALU.mult)
                    DVE.call("tensor_tensor", out=mb[:], in0=mb[:], in1=uc[:], op=ALU.mult)
                    hdst = hf if d_ == 0 else hb
                    if ps == 0:
                        init = 0.0
                    else:
                        hcol = hin[:, d_:d_ + 1]
                        tcol = hin[:, 4:5]
                        DVE.call("memset", hcol, 0.0)
                        order = range(4) if d_ == 0 else range(3, -1, -1)
                        for r in order:
                            A_r = sgt[:, r, d_, j, 0:1]
                            B_r = sgt[:, r, d_, j, 1:2]
                            DVE.call("scalar_tensor_tensor", out=tcol, in0=hcol, scalar=A_r, in1=B_r, op0=ALU.mult, op1=ALU.add)
                            DVE.call("tensor_tensor", out=tcol, in0=tcol, in1=hcol, op=ALU.subtract)
                            DVE.call("scalar_tensor_tensor", out=hcol, in0=tcol, scalar=mask(d_, r), in1=hcol,
                                     op0=ALU.mult, op1=ALU.add)
                        init = hcol
                    if d_ == 0:
                        DVE.call("tensor_tensor_scan", out=hdst[:, 0:NTOK], data0=rb[:], data1=mb[:], initial=init,
                                 op0=ALU.mult, op1=ALU.add)
                        if ps == 0:
                            DVE.call("tensor_copy", out=sumst[:, d_, j, 1:2], in_=hdst[:, NTOK - 1:NTOK])
                    else:
                        DVE.call("tensor_tensor_scan", out=hdst[:, ::-1], data0=rb[:, ::-1], data1=mb[:, ::-1], initial=init,
                                 op0=ALU.mult, op1=ALU.add)
                        if ps == 0:
                            DVE.call("tensor_copy", out=sumst[:, d_, j, 1:2], in_=hdst[:, 0:1])
                    dserial(ACT, DVE, PE)
                if ps == 1:
                    dserial(SP, ACT, DVE)
                    SP.wait(Tok(DVE.sem, DVE.count))
                    SP.wait(Tok(s_dy.sem, s_dy.n))
                    tl = SP.dma(s_dy, ig[:], ygT_scr[j])
                    DVE.wait(tl)
                    DVE.call("tensor_tensor", out=hf[:], in0=hf[:], in1=hb, op=ALU.add)
                    td = DVE.call("tensor_tensor", out=ucb[:], in0=hf[:], in1=ig[:], op=ALU.mult)
                    POOL.wait(td)
                    tst = POOL.dma(s_dst, mixT_scr[8 + j], ucb[:])
                    DVE.wait(tst)
                    ACT.wait(tst)
            if ps == 0:
                POOL.wait(Tok(DVE.sem, DVE.count), Tok(ACT.sem, ACT.count))
                tst = POOL.dma(s_dst, sum_loc.rearrange("(d j p) c -> p d j c", d=2, j=8), sumst[:])
                POOL.wait(tst)
                allgather(sum_loc2 if not NO_CC else sum_loc, sum_g2 if not NO_CC else sum_g, 2048)
                barrier()

        lru_pass(0)
        lru_pass(1)
        barrier()


    if 'E' in PHASES:
        aoff[0] = aoff_AE
        pt_f = carve("pt_f", [128, 4, 256], F32)
        pb = carve("pb", [128, 4, 256], BF16)
        pT = carve("pT", [128, 2, TT], BF16)
        s_p = dsem("d_p")
        s_m = dsem("d_m")
        out_tiles = out.rearrange("(i ts p) f -> i p ts f", ts=4, p=128)
        p_tiles = p_in.rearrange("(i ts p) f -> i p ts f", ts=4, p=128)
        for R_ in (xt_res, xb_res, junk_res, aT_r, aT_w, hT_w[0], hT_w[1], hT_r[0], hT_r[1]):
            R_.t = {}
        for b_ in banks:
            b_.free = []
        ring_free[:] = [None] * NR
        silu_free[:] = [None, None]
        t_out = None
        for i in range(NTILE):
            POOL.wait(xt_res, t_out)
            tx = POOL.dma(s_x, xt[:], x1_tiles[i])
            xt_res.set(tx)
            SP.wait(hT_r[1])
            tm = SP.dma(s_m, hT[1][:], mixT_scr[:, :, i * TT:(i + 1) * TT].rearrange("c d t -> d c t"))
            hT_w[1] = Res()
            hT_w[1].add(tm)
            SP.wait(Tok(DVE.sem, DVE.count))
            tpl = SP.dma(s_p, pt_f[:], p_tiles[i])

            def o_epi(fg, bk, gtok):
                for ts in range(4):
                    DVE.wait(gtok, xt_res)
                    sl = xt[:, ts, fg * 512:(fg + 1) * 512]
                    td = DVE.call("tensor_tensor", out=sl, in0=bk[ts].ap, in1=sl, op=ALU.add)
                    bk[ts].free = [td]
                    xt_res.add(td)

            lt = proj_T(lambda kc, ts: hT[1][:, kc, ts * 128:(ts + 1) * 128], hT_w[1], 16, wout_b, 4, cast_tok["wout"], o_epi)
            hT_r[1].add(lt)
            tp = norm_front()
            norm_scale(tp)
            norm_back(0, C_G2)
            ffn(0, wup_b, wdn_b, cast_tok["upb"], cast_tok["dnb"])
            tp = norm_front()
            norm_scale(tp)
            norm_back(1, C_GP)
            DVE.wait(tpl)
            tdp = DVE.call("tensor_copy", out=pb[:], in_=pt_f[:])
            pT_tok = None
            for kc in range(2):
                bk = nb()
                pe_take(bk)
                PE.wait(tdp)
                tok = None
                for ts in range(4):
                    tok = PE.call("transpose", out=bk.bf[:, ts * 128:(ts + 1) * 128], in_=pb[:, ts, kc * 128:(kc + 1) * 128],
                                  identity=ident_b[:], mark=(ts == 3))
                DVE.wait(tok)
                pT_tok = DVE.call("tensor_copy", out=pT[:, kc, :], in_=bk.bf[:, 0:TT])
                bk.free = [pT_tok]
            for fg in range(4):
                gate_bk = []

                def g_epi(cg, bk, gtok):
                    gate_bk.append((bk, gtok))

                lt = proj_T(lambda kc, ts: hT[1][:, kc, ts * 128:(ts + 1) * 128], hT_w[1], 16, wg_b, 4, cast_tok["wg"], g_epi,
                            cg_list=[fg])
                hT_r[1].add(lt)

                def pp_epi(cg, bk, gtok):
                    gb, gt = gate_bk[0]
                    for ts in range(4):
                        ACT.wait(gt)
                        ta = ACT.call("activation", out=silu_tmp[0][:], in_=gb[ts].ap, func=AF.Sigmoid)
                        gb[ts].free = [ta]
                        DVE.wait(ta, gtok)
                        td = DVE.call("tensor_tensor", out=silu_tmp[1][:], in0=silu_tmp[0][:], in1=bk[ts].ap, op=ALU.mult)
                        bk[ts].free = [td]
                        ACT.wait(td)
                        sl = xt[:, ts, cg * 512:(cg + 1) * 512]
                        td = DVE.call("tensor_tensor", out=sl, in0=sl, in1=silu_tmp[1][:], op=ALU.add)
                        xt_res.add(td)

                proj_T(lambda kc, ts: pT[:, kc, ts * 128:(ts + 1) * 128], pT_tok, 2, wpp_b, 4, cast_tok["wpp"], pp_epi, ku=2,
                       cg_list=[fg])
            tp = norm_front()
            for ts in range(4):
                DVE.wait(tp, xt_res)
                td = DVE.call("scalar_tensor_tensor", out=xt[:, ts, :], in0=xt[:, ts, :], scalar=rstd[:, ts:ts + 1],
                              in1=rows[:, 0:D], op0=ALU.mult, op1=ALU.mult)
                xt_res.add(td)
            POOL.wait(xt_res)
            t_out = POOL.dma(s_st, out_tiles[i], xt[:])
            xt_res.add(t_out)


    POOL.wait([Tok(d.sem, d.n) for d in all_store_sems if d.n > 0])

    with nc.allow_non_contiguous_dma(reason="tiny halo / edge transfers"), nc.Block() as block:
        @block.sync
        def _(e):
            SP.replay(e)

        @block.scalar
        def _(e):
            ACT.replay(e)

        @block.vector
        def _(e):
            DVE.replay(e)

        @block.gpsimd
        def _(e):
            POOL.replay(e)

        @block.tensor
        def _(e):
            PE.replay(e)
    es.close()
    return nc


def _pc(v, nchunk):
    return np.ascontiguousarray(np.asarray(v, np.float32).reshape(nchunk, 128).T)


def _prep_inputs(inp):
    f = lambda k: np.ascontiguousarray(np.asarray(inp[k], np.float32))
    x = f("x")
    p = f("p")[0]
    shared = {
        "w1a": f("w1_ffn1")[0], "w3a": f("w3_ffn1")[0], "w2a": f("w2_ffn1")[0],
        "win": f("w_in")[0], "wout": f("w_out")[0],
        "w1b": f("w1_ffn2")[0], "w3b": f("w3_ffn2")[0], "w2b": f("w2_ffn2")[0],
        "wg": f("w_ple_gate")[0], "wpp": f("w_ple_proj")[0],
        "lwa": f("lru_wa")[0].reshape(2048, 128), "lwi": f("lru_wi")[0].reshape(2048, 128),
        "ident": np.eye(128, dtype=np.float32),
    }
    rows = np.concatenate([f("norm_final"), f("q_norm")[0], f("k_norm")[0]])[None, :]
    shared["rows"] = np.ascontiguousarray(np.broadcast_to(rows, (128, 2304)))
    sm = np.zeros((128, NS), np.float32)
    sm[:, C_G1:C_G1 + 16] = _pc(f("norm_ffn1")[0], 16)
    sm[:, C_GM:C_GM + 16] = _pc(f("norm_mix")[0], 16)
    sm[:, C_G2:C_G2 + 16] = _pc(f("norm_ffn2")[0], 16)
    sm[:, C_GP:C_GP + 16] = _pc(f("norm_ple")[0], 16)
    cw = f("conv_w")[0]
    for tap in range(4):
        sm[:, C_CW + tap * 8:C_CW + tap * 8 + 8] = _pc(cw[tap], 8)
    sm[:, C_CB:C_CB + 8] = _pc(f("conv_b")[0], 8)
    sm[:, C_BA:C_BA + 16] = _pc(f("lru_ba")[0].reshape(-1), 16)
    sm[:, C_BI:C_BI + 16] = _pc(f("lru_bi")[0].reshape(-1), 16)
    sm[:, C_LAM:C_LAM + 16] = _pc(f("lru_lambda")[0].reshape(-1), 16)
    t = np.arange(SEQ)
    inv = np.float32(10000.0) ** (-np.arange(0, 64, 2, dtype=np.float32) / np.float32(64))
    ang = np.concatenate([(t // 64).astype(np.float32)[:, None] * inv,
                          (t % 64).astype(np.float32)[:, None] * inv], axis=-1).astype(np.float32)
    cs_full = np.concatenate([np.cos(ang), np.sin(ang)], axis=-1).astype(np.float32)
    in_maps = []
    for c in range(8):
        b, sc = c // 4, c % 4
        m = dict(shared)
        m["x"] = np.ascontiguousarray(x[b, sc * NTOK:(sc + 1) * NTOK])
        m["p"] = np.ascontiguousarray(p[b, sc * NTOK:(sc + 1) * NTOK])
        m["cs"] = np.ascontiguousarray(cs_full[sc * NTOK:(sc + 1) * NTOK])
        s2 = sm.copy()
        for r in range(4):
            s2[:, C_MASK + r] = 1.0 if r < sc else 0.0
            s2[:, C_MASK + 4 + r] = 1.0 if r > sc else 0.0
            s2[:, C_MASK + 8 + r] = 1.0 if r == sc - 1 else 0.0
            s2[:, C_MASK + 12 + r] = 1.0 if r == sc + 1 else 0.0
        m["smalls"] = s2
        in_maps.append(m)
    return in_maps


def kernel(**inputs):
    in_maps = _prep_inputs(inputs)
    nc = build_nc()
    res = run_bass_kernel_spmd(nc, in_maps, core_ids=list(range(8)))
    outs = [np.asarray(r["out"]) for r in res.results]
    full = np.stack([np.concatenate(outs[0:4], axis=0), np.concatenate(outs[4:8], axis=0)], axis=0)
    return full.astype(np.float32)
```

```python
import math
from contextlib import ExitStack

import numpy as np
import concourse.bass as bass
import concourse.mybir as mybir
from concourse.bass_utils import run_bass_kernel_spmd

F32 = mybir.dt.float32
BF16 = mybir.dt.bfloat16
AF = mybir.ActivationFunctionType
ALU = mybir.AluOpType
AX = mybir.AxisListType

D = 2048
DFF = 5632
NTOK = 4096
TT = 512
NTILE = NTOK // TT
SEQ = 16384
EPS = 1e-6
NS = 168
C_G1, C_GM, C_G2, C_GP = 0, 16, 32, 48
C_CW, C_CB, C_BA, C_BI, C_LAM, C_MASK = 64, 96, 104, 120, 136, 152
GELU_K = 2.0 * math.sqrt(2.0 / math.pi)
DBG = False
NO_CC = False
NKC = 128
PHASES = 'ACDE'
CUT = 0


class StopEmit(Exception):
    pass


def cp(k):
    if CUT == k:
        raise StopEmit()


class Tok:
    __slots__ = ("sem", "n")

    def __init__(self, sem, n):
        self.sem = sem
        self.n = n


class Res:
    def __init__(self):
        self.t = {}

    def add(self, tok):
        if tok is None:
            return
        k = id(tok.sem)
        if k not in self.t or self.t[k].n < tok.n:
            self.t[k] = tok

    def set(self, tok):
        self.t = {}
        self.add(tok)


class DSem:
    def __init__(self, sem):
        self.sem = sem
        self.n = 0


class Stream:
    def __init__(self, name):
        self.name = name
        self.sem = None
        self.ops = []
        self.count = 0
        self.waited = {}
        self.serial = False

    def wait(self, *toks):
        for t in toks:
            if t is None:
                continue
            if isinstance(t, Res):
                self.wait(*t.t.values())
                continue
            if isinstance(t, (list, tuple)):
                self.wait(*t)
                continue
            k = id(t.sem)
            if self.waited.get(k, 0) >= t.n:
                continue
            self.waited[k] = t.n
            self.ops.append((0, t.sem, t.n))

    def call(self, meth, *a, mark=True, **kw):
        fn = (lambda e, meth=meth, a=a, kw=kw: getattr(e, meth)(*a, **kw))
        if self.serial:
            mark = True
            if self.count > 0:
                self.wait(Tok(self.sem, self.count))
        if mark:
            self.count += 1
            self.ops.append((1, fn, (self.sem, 1)))
            return Tok(self.sem, self.count)
        self.ops.append((1, fn, None))
        return None

    def dma(self, dsem, out, in_):
        dsem.n += 16
        fn = (lambda e, o=out, i=in_: e.dma_start(out=o, in_=i))
        self.ops.append((1, fn, (dsem.sem, 16)))
        return Tok(dsem.sem, dsem.n)

    def replay(self, e):
        for o in self.ops:
            if o[0] == 0:
                e.wait_ge(o[1], o[2])
            else:
                ins = o[1](e)
                if o[2] is not None:
                    ins.then_inc(o[2][0], o[2][1])


class Bank:
    def __init__(self, ap):
        self.ap = ap
        self.bf = ap.bitcast(BF16)
        self.free = []


def build_nc():
    nc = bass.Bass("TRN2", target_bir_lowering=False)
    es = ExitStack()

    def din(name, shape, dt=F32):
        return nc.dram_tensor(name, shape, dt, kind="ExternalInput").ap()

    def dsc(name, shape, dt, dbg=False):
        if dbg and DBG:
            return nc.dram_tensor(name, shape, dt, kind="ExternalOutput").ap()
        return nc.dram_tensor(name, shape, dt).ap()

    x_in = din("x", [NTOK, D])
    p_in = din("p", [NTOK, 256])
    w_f = {
        "w1a": din("w1a", [D, DFF]), "w3a": din("w3a", [D, DFF]), "w2a": din("w2a", [DFF, D]),
        "win": din("win", [D, 3584]), "wout": din("wout", [D, D]),
        "w1b": din("w1b", [D, DFF]), "w3b": din("w3b", [D, DFF]), "w2b": din("w2b", [DFF, D]),
        "wg": din("wg", [D, D]), "wpp": din("wpp", [256, D]),
    }
    lwa_in = din("lwa", [2048, 128])
    lwi_in = din("lwi", [2048, 128])
    smalls_in = din("smalls", [128, NS])
    rows_in = din("rows", [128, 2304])
    cs_in = din("cs", [NTOK, 128])
    ident_in = din("ident", [128, 128])
    out = nc.dram_tensor("out", [NTOK, D], F32, kind="ExternalOutput").ap()

    wup_a = dsc("wup_a", [88, 128, 2048], BF16)
    wdn_a = dsc("wdn_a", [44, 128, 2048], BF16)
    win_b = dsc("win_b", [28, 128, 2048], BF16)
    wout_b = dsc("wout_b", [16, 128, 2048], BF16)
    wup_b = dsc("wup_b", [88, 128, 2048], BF16)
    wdn_b = dsc("wdn_b", [44, 128, 2048], BF16)
    wg_b = dsc("wg_b", [16, 128, 2048], BF16)
    wpp_b = dsc("wpp_b", [4, 128, 1024], BF16)
    lw_b = dsc("lw_b", [2, 2048, 128], BF16)

    x1_scr = dsc("x1_scr", [NTOK, D], F32, dbg=True)
    qT_scr = dsc("qT_scr", [2, 32, 128, 512], BF16, dbg=True)
    kv_loc = dsc("kv_loc", [512, 4096], BF16)
    kv_g = dsc("kv_g", [2048, 4096], BF16)
    uT_scr = dsc("uT_scr", [8, 128, NTOK], F32, dbg=True)
    ygT_scr = dsc("ygT_scr", [8, 128, NTOK], F32, dbg=True)
    halo_loc2 = dsc("halo_loc", [16, 512], F32)
    halo_g2 = dsc("halo_g", [64, 512], F32)
    sum_loc2 = dsc("sum_loc", [32, 512], F32)
    sum_g2 = dsc("sum_g", [128, 512], F32)
    halo_loc = halo_loc2.rearrange("a (b c) -> (a b) c", c=8)
    halo_g = halo_g2.rearrange("a (b c) -> (a b) c", c=8)
    sum_loc = sum_loc2.rearrange("a (b c) -> (a b) c", c=8)
    sum_g = sum_g2.rearrange("a (b c) -> (a b) c", c=8)
    mixT_scr = dsc("mixT_scr", [16, 128, NTOK], BF16, dbg=True)
    if DBG:
        dbg_kv = nc.dram_tensor("dbg_kv", [512, 4096], BF16, kind="ExternalOutput").ap()

    SP, ACT, DVE, POOL, PE = (Stream(n) for n in ("sp", "act", "dve", "pool", "pe"))
    for s in (ACT, DVE, POOL, PE):
        s.sem = es.enter_context(nc.semaphore("m_" + s.name))
    for s in (ACT, DVE, POOL):
        s.serial = True

    dsem_list = []

    def dsem(name):
        d_ = DSem(es.enter_context(nc.semaphore(name)))
        dsem_list.append(d_)
        return d_

    def sb(name, shape, dt):
        return es.enter_context(nc.sbuf_tensor("sb_" + name, shape, dt))

    ARENA_KB = 194
    arena = es.enter_context(nc.sbuf_tensor("arena", [128, ARENA_KB * 256], F32))
    aoff = [0]

    def carve(name, shape, dt):
        n = 1
        for d_ in shape[1:]:
            n *= d_
        nb_ = n * (2 if dt == BF16 else 4)
        nb_ = (nb_ + 31) // 32 * 32
        o = aoff[0]
        aoff[0] += nb_
        assert aoff[0] <= ARENA_KB * 1024, (name, aoff[0])
        ap = arena[:, o // 4:(o + nb_) // 4]
        if dt == BF16:
            ap = ap.bitcast(BF16)
        ap = ap[:, 0:n]
        if len(shape) == 3:
            ap = ap.rearrange("p (a b) -> p a b", a=shape[1])
        elif len(shape) == 4:
            ap = ap.rearrange("p (a b c) -> p a b c", a=shape[1], b=shape[2])
        elif len(shape) == 5:
            ap = ap.rearrange("p (a b c d) -> p a b c d", a=shape[1], b=shape[2], c=shape[3])
        return ap

    smalls = sb("smalls", [128, NS], F32)
    rows = sb("rows", [128, 2304], F32)
    ident_f = sb("ident_f", [128, 128], F32)
    ident_b = sb("ident_b", [128, 128], BF16)
    ones_b = sb("ones_b", [128, 128], BF16)
    neghalf = sb("neghalf", [128, 8], F32)
    cdec = sb("cdec", [128, 16], F32)
    cdec2 = sb("cdec2", [128, 16], F32)
    tsm = sb("tsm", [128, 8, 16], F32)
    s_const = dsem("d_const")

    dbk = [es.enter_context(nc.psum_tensor(f"dbk{i}", [128, 1024], F32)) for i in range(4)]
    banks = [Bank(dbk[j // 2][:, (j % 2) * 512:(j % 2 + 1) * 512]) for j in range(8)]
    nbc = [0]

    def nb():
        b = banks[nbc[0] % 8]
        nbc[0] += 1
        return b

    def pe_take(b):
        PE.wait(b.free)
        b.free = []

    t_c = [SP.dma(s_const, smalls[:], smalls_in), SP.dma(s_const, rows[:], rows_in),
           SP.dma(s_const, ident_f[:], ident_in)]
    c_tok = t_c[-1]
    DVE.wait(c_tok)
    DVE.call("tensor_copy", out=ident_b[:], in_=ident_f[:])
    DVE.call("memset", ones_b[:], 1.0)
    k0 = DVE.call("memset", neghalf[:], -0.5)
    lam = smalls[:, C_LAM:C_LAM + 16]
    T = lambda i: tsm[:, i, :]
    DVE.wait(k0)
    k1 = DVE.call("tensor_scalar", out=T(7), in0=lam, scalar1=-1.0, scalar2=None, op0=ALU.mult)
    DVE.wait(k1)
    k1 = DVE.call("tensor_tensor", out=T(0), in0=lam, in1=T(7), op=ALU.max)
    ACT.wait(k1)
    k2 = ACT.call("activation", out=T(1), in_=T(0), func=AF.Exp, scale=-1.0)
    ACT.wait(k2)
    k3 = ACT.call("activation", out=T(2), in_=T(1), func=AF.Ln, bias=1.0)
    DVE.wait(k3)
    DVE.call("tensor_scalar", out=T(3), in0=T(1), scalar1=-0.25, scalar2=1.0 / 3.0, op0=ALU.mult, op1=ALU.add)
    k4 = DVE.call("tensor_tensor", out=T(3), in0=T(3), in1=T(1), op=ALU.mult)
    DVE.wait(k4)
    k4 = DVE.call("tensor_scalar", out=T(3), in0=T(3), scalar1=-0.5, scalar2=None, op0=ALU.add)
    DVE.wait(k4)
    k4 = DVE.call("tensor_tensor", out=T(3), in0=T(3), in1=T(1), op=ALU.mult)
    DVE.wait(k4)
    k4 = DVE.call("tensor_scalar", out=T(3), in0=T(3), scalar1=1.0, scalar2=None, op0=ALU.add)
    DVE.wait(k4)
    k4 = DVE.call("tensor_tensor", out=T(3), in0=T(3), in1=T(1), op=ALU.mult)
    DVE.wait(k4)
    k5 = DVE.call("tensor_scalar", out=T(4), in0=T(1), scalar1=-100.0, scalar2=6.0, op0=ALU.mult, op1=ALU.add)
    DVE.wait(k5)
    k5 = DVE.call("tensor_scalar", out=T(4), in0=T(4), scalar1=0.0, scalar2=1.0, op0=ALU.max, op1=ALU.min)
    DVE.wait(k5)
    k6 = DVE.call("tensor_tensor", out=T(5), in0=T(3), in1=T(2), op=ALU.subtract)
    DVE.wait(k6)
    k6 = DVE.call("tensor_tensor", out=T(5), in0=T(5), in1=T(4), op=ALU.mult)
    DVE.wait(k6)
    k6 = DVE.call("tensor_tensor", out=T(5), in0=T(5), in1=T(2), op=ALU.add)
    DVE.wait(k6)
    k7 = DVE.call("tensor_scalar", out=T(6), in0=lam, scalar1=-1.0, scalar2=0.0, op0=ALU.mult, op1=ALU.max)
    DVE.wait(k7)
    k7 = DVE.call("tensor_tensor", out=T(6), in0=T(6), in1=T(5), op=ALU.add)
    DVE.wait(k7)
    k_cdec = DVE.call("tensor_scalar", out=cdec[:], in0=T(6), scalar1=-8.0, scalar2=None, op0=ALU.mult)
    DVE.wait(k_cdec)
    k_cdec = DVE.call("tensor_scalar", out=cdec2[:], in0=cdec[:], scalar1=2.0, scalar2=None, op0=ALU.mult)
    const_res = Res()
    const_res.add(k_cdec)
    const_res.add(c_tok)

    cast_sems = {}
    cast_tok = {}

    def cast_dma(key, out_ap, in_ap):
        if key not in cast_sems:
            cast_sems[key] = dsem("c_" + key)
        cast_tok[key] = POOL.dma(cast_sems[key], out_ap, in_ap)

    def cast_up(key, dst, w1, w3):
        for cg in range(22):
            for kq in range(4):
                ov = dst[cg * 4 + kq].rearrange("p (k w c) -> p k w c", k=4, w=2)
                for wi, w in enumerate((w1, w3)):
                    iv = w[kq * 512:(kq + 1) * 512, cg * 256:(cg + 1) * 256].rearrange("(k p) c -> p k c", p=128)
                    cast_dma(key, ov[:, :, wi, :], iv)

    def cast_T(key, dst, w, n_cg, n_kq, ku=4):
        for cg in range(n_cg):
            for kq in range(n_kq):
                ov = dst[cg * n_kq + kq].rearrange("p (k c) -> p k c", k=ku)
                iv = w[kq * ku * 128:(kq + 1) * ku * 128, cg * 512:(cg + 1) * 512].rearrange("(k p) c -> p k c", p=128)
                cast_dma(key, ov, iv)

    late_casts = []

    xt = carve("xt", [128, 4, D], F32)
    xb = carve("xb", [128, 4, D], BF16)
    hT = [carve("hT0", [128, 16, TT], BF16), carve("hT1", [128, 16, TT], BF16)]
    aT = carve("aT", [128, 44, TT], BF16)
    NR = 8
    ring = [carve(f"ring{i}", [128, 2048], BF16) for i in range(NR)]
    ring_sem = [dsem(f"d_ring{i}") for i in range(NR)]
    ring_free = [None] * NR
    ucount = [0]
    silu_tmp = [carve("silu0", [128, TT], F32), carve("silu1", [128, TT], F32)]
    silu_free = [None, None]
    silu_i = [0]
    ss = sb("ss", [128, 8], F32)
    ms = sb("ms", [128, 8], F32)
    rstd = sb("rstd", [128, 8], F32)
    s_x = dsem("d_x")
    s_st = dsem("d_store")
    all_store_sems = [s_st]

    def stsem(name):
        d = dsem(name)
        all_store_sems.append(d)
        return d
    xt_res, xb_res, junk_res, st_res = Res(), Res(), Res(), Res()
    hT_w = [Res(), Res()]
    hT_r = [Res(), Res()]
    aT_r = Res()
    aT_w = Res()

    def barrier():
        toks = [Tok(S.sem, S.count) for S in (ACT, DVE, POOL, PE) if S.count > 0]
        for S in (SP, ACT, DVE, POOL, PE):
            S.wait(toks)
            S.wait([Tok(d.sem, d.n) for d in dsem_list if d.n > 0])
            if cc_state[0] > 0:
                S.wait(Tok(s_cc, cc_state[0]))

    cc_state = [0]
    s_cc = es.enter_context(nc.semaphore("cc"))

    def load_unit(src, n, ctok):
        s = ucount[0] % NR
        ucount[0] += 1
        SP.wait(ring_free[s], ctok)
        tl = SP.dma(ring_sem[s], ring[s][:, 0:n], src)
        return s, tl

    def proj_T(act, act_ready, n_kc, src, n_cg, ctok, epilogue, ku=4, cg_list=None):
        nkq = n_kc // ku
        last_tok = None
        for cg in (cg_list if cg_list is not None else range(n_cg)):
            bk = [nb() for _ in range(4)]
            for b in bk:
                pe_take(b)
            PE.wait(act_ready)
            for kq in range(nkq):
                s, tl = load_unit(src[cg * nkq + kq], ku * 512, ctok)
                PE.wait(tl)
                v = ring[s][:, 0:ku * 512].rearrange("p (k c) -> p k c", k=ku)
                for kcl in range(ku):
                    kc = kq * ku + kcl
                    for ts in range(4):
                        fin = (kcl == ku - 1 and ts == 3)
                        tok = PE.call("matmul", bk[ts].ap, act(kc, ts), v[:, kcl, :],
                                      start=(kc == 0), stop=(kc == n_kc - 1), mark=fin)
                        if fin:
                            ring_free[s] = tok
                            last_tok = tok
            epilogue(cg, bk, last_tok)
        return last_tok

    def proj_F(act, act_ready, n_kc, src, cg_list, ctok, epilogue, dual):
        nkq = n_kc // 4
        last_tok = None
        for ci, cg in enumerate(cg_list):
            bk = [nb() for _ in range(4)]
            for b in bk:
                pe_take(b)
            PE.wait(act_ready)
            for kq in range(nkq):
                s, tl = load_unit(src[cg * nkq + kq], 2048, ctok)
                PE.wait(tl)
                if dual:
                    v = ring[s][:].rearrange("p (k w c) -> p k w c", k=4, w=2)
                else:
                    v = ring[s][:].rearrange("p (k c) -> p k c", k=4)
                for kcl in range(4):
                    kc = kq * 4 + kcl
                    for j in range(4):
                        if dual:
                            lhsT = v[:, kcl, j // 2, (j % 2) * 128:(j % 2 + 1) * 128]
                        else:
                            lhsT = v[:, kcl, j * 128:(j + 1) * 128]
                        fin = (kcl == 3 and j == 3)
                        tok = PE.call("matmul", bk[j].ap, lhsT, act(kc),
                                      start=(kc == 0), stop=(kc == n_kc - 1), mark=fin)
                        if fin:
                            ring_free[s] = tok
                            last_tok = tok
            epilogue(cg, bk, last_tok)
        return last_tok

    def norm_front(nsub=4):
        ta = None
        for ts in range(nsub):
            ACT.wait(xt_res, xb_res)
            ta = ACT.call("activation", out=xb[:, ts, :], in_=xt[:, ts, :], func=AF.Square, accum_out=ss[:, ts:ts + 1])
            xb_res.add(ta)
        xt_res.add(ta)
        DVE.wait(ta)
        td = DVE.call("tensor_scalar", out=ms[:, 0:4], in0=ss[:, 0:4], scalar1=1.0 / D, scalar2=EPS,
                      op0=ALU.mult, op1=ALU.add)
        ACT.wait(td)
        tq = ACT.call("activation", out=ms[:, 4:8], in_=ms[:, 0:4], func=AF.Sqrt)
        DVE.wait(tq)
        tp = DVE.call("reciprocal", out=rstd[:, 0:4], in_=ms[:, 4:8])
        return tp

    def norm_scale(tp):
        for ts in range(4):
            DVE.wait(tp, xb_res, xt_res)
            td = DVE.call("tensor_scalar", out=xb[:, ts, :], in0=xt[:, ts, :], scalar1=rstd[:, ts:ts + 1],
                          scalar2=None, op0=ALU.mult)
            xb_res.add(td)
            xt_res.add(td)

    def norm_back(b, gcol):
        hT_w[b] = Res()
        for kc in range(16):
            bk = nb()
            pe_take(bk)
            PE.wait(xb_res)
            tok = None
            for ts in range(4):
                tok = PE.call("transpose", out=bk.bf[:, ts * 128:(ts + 1) * 128],
                              in_=xb[:, ts, kc * 128:(kc + 1) * 128], identity=ident_b[:], mark=(ts == 3))
            xb_res.add(tok)
            DVE.wait(tok, hT_r[b])
            te = DVE.call("tensor_scalar", out=hT[b][:, kc, :], in0=bk.bf[:, 0:TT],
                          scalar1=smalls[:, gcol + kc:gcol + kc + 1], scalar2=None, op0=ALU.mult)
            bk.free = [te]
            hT_w[b].add(te)
        hT_r[b] = Res()

    def ffn(b, up_src, dn_src, ctok_up, ctok_dn):
        def up_epi(cg, bk, gtok):
            for mi in range(2):
                m = cg * 2 + mi
                A, Bk = bk[mi], bk[2 + mi]
                si = silu_i[0] % 2
                silu_i[0] += 1
                ACT.wait(gtok, silu_free[si])
                ta = ACT.call("activation", out=silu_tmp[si][:], in_=A.ap, func=AF.Silu)
                A.free = [ta]
                DVE.wait(ta, gtok, aT_r)
                td = DVE.call("tensor_tensor", out=aT[:, m, :], in0=silu_tmp[si][:], in1=Bk.ap, op=ALU.mult)
                Bk.free = [td]
                silu_free[si] = td
                aT_w.add(td)

        lt = proj_F(lambda kc: hT[b][:, kc, :], hT_w[b], 16, up_src, list(range(22)), ctok_up, up_epi, True)
        hT_r[b].add(lt)

        def dn_epi(fg, bk, gtok):
            for ts in range(4):
                DVE.wait(gtok, xt_res)
                sl = xt[:, ts, fg * 512:(fg + 1) * 512]
                td = DVE.call("scalar_tensor_tensor", out=sl, in0=bk[ts].ap, scalar=0.5, in1=sl,
                              op0=ALU.mult, op1=ALU.add)
                bk[ts].free = [td]
                xt_new.add(td)

        xt_new = Res()
        lt = proj_T(lambda kc, ts: aT[:, kc, ts * 128:(ts + 1) * 128], aT_w, 44, dn_src, 4, ctok_dn, dn_epi)
        aT_r.set(lt)
        for t in xt_new.t.values():
            xt_res.add(t)

    qf = carve("qf", [128, 4, 128], F32)
    qr4 = [carve(f"qr{i}", [128, 4, 128], BF16) for i in range(4)]
    rt = [carve(f"rt{i}", [128, 4, 64], F32) for i in range(4)]
    ssq = sb("ssq", [128, 8], F32)
    msq = sb("msq", [128, 8], F32)
    rsq = sb("rsq", [128, 8], F32)
    junk2 = carve("junk2", [128, 128], F32)
    cs_t = carve("cs_t", [128, 4, 128], F32)
    qT_st = [carve("qTst0", [128, 512], BF16), carve("qTst1", [128, 512], BF16)]
    vb = carve("vb", [128, 256], BF16)
    ust = [carve("ust0", [128, TT], F32), carve("ust1", [128, TT], F32)]
    yt = [carve(f"yt{i}", [128, TT], F32) for i in range(3)]
    s_cs = dsem("d_cs")
    qst_sem = [stsem("d_qst0"), stsem("d_qst1")]
    vb_sem = stsem("d_vb")
    ust_sem = [stsem("d_ust0"), stsem("d_ust1")]
    yt_sem = stsem("d_yt")
    qf_res, cs_res, vb_res = Res(), Res(), Res()
    qr_res4 = [Res() for _ in range(4)]
    qTst_res = [Res(), Res()]
    ust_res = [Res(), Res()]
    yt_res = Res()
    qst_i = [0]
    ust_i = [0]
    pend_tr = []

    aoff_AE = aoff[0]
    kvl_k = kv_loc[0:256, :]
    kvl_v = kv_loc[256:512, :].rearrange("(g p a) (b d) -> g p (a b) d", g=2, p=128, d=128)

    def qk_epilogue(i, cg, ts, bank, gtok):
        nh = 4 if cg < 2 else 2
        gain = rows[:, 2048:2176] if cg < 2 else rows[:, 2176:2304]
        W = nh * 128
        ACT.wait(gtok, qf_res)
        ta = ACT.call("activation", out=qf[:, 0:nh, :], in_=bank.ap[:, 0:W].rearrange("p (h d) -> p h d", h=nh),
                      func=AF.Copy)
        if cg == 2:
            ACT.wait(vb_res)
            tv = ACT.call("activation", out=vb[:], in_=bank.ap[:, 256:512], func=AF.Copy)
            vb_res.set(tv)
            bank.free = [tv]
            ch = i * 4 + ts
            POOL.wait(tv)
            tvs = POOL.dma(vb_sem, kvl_v[:, :, ch, :].rearrange("g p d -> p g d"),
                           vb[:].rearrange("p (g d) -> p g d", g=2))
            vb_res.add(tvs)
        else:
            bank.free = [ta]
        for hh in range(nh):
            ACT.wait(ta)
            ta2 = ACT.call("activation", out=junk2[:], in_=qf[:, hh, :], func=AF.Square,
                           accum_out=ssq[:, hh:hh + 1])
        DVE.wait(ta2)
        td = DVE.call("tensor_scalar", out=msq[:, 0:nh], in0=ssq[:, 0:nh], scalar1=1.0 / 128, scalar2=EPS,
                      op0=ALU.mult, op1=ALU.add)
        ACT.wait(td)
        tq = ACT.call("activation", out=msq[:, 4:4 + nh], in_=msq[:, 0:nh], func=AF.Sqrt)
        DVE.wait(tq)
        tp = DVE.call("reciprocal", out=rsq[:, 0:nh], in_=msq[:, 4:4 + nh])
        DVE.wait(tp)
        td = DVE.call("tensor_tensor", out=qf[:, 0:nh, :], in0=qf[:, 0:nh, :],
                      in1=rsq[:, 0:nh].unsqueeze(2).to_broadcast([128, nh, 128]), op=ALU.mult)
        DVE.wait(td)
        td = DVE.call("tensor_tensor", out=qf[:, 0:nh, :], in0=qf[:, 0:nh, :],
                      in1=gain.unsqueeze(1).to_broadcast([128, nh, 128]), op=ALU.mult)
        qr = qr4[ts]
        DVE.wait(td, cs_res, qr_res4[ts])
        x1 = qf[:, 0:nh, 0::2]
        x2 = qf[:, 0:nh, 1::2]
        cc = cs_t[:, ts, 0:64].unsqueeze(1).to_broadcast([128, nh, 64])
        sn = cs_t[:, ts, 64:128].unsqueeze(1).to_broadcast([128, nh, 64])
        DVE.call("tensor_tensor", out=rt[0][:, 0:nh, :], in0=x1, in1=cc, op=ALU.mult, mark=False)
        DVE.call("tensor_tensor", out=rt[1][:, 0:nh, :], in0=x2, in1=sn, op=ALU.mult, mark=False)
        DVE.call("tensor_tensor", out=rt[2][:, 0:nh, :], in0=x1, in1=sn, op=ALU.mult, mark=False)
        td = DVE.call("tensor_tensor", out=rt[3][:, 0:nh, :], in0=x2, in1=cc, op=ALU.mult)
        DVE.wait(td)
        DVE.call("tensor_tensor", out=qr[:, 0:nh, 0::2], in0=rt[0][:, 0:nh, :], in1=rt[1][:, 0:nh, :],
                 op=ALU.subtract, mark=False)
        td = DVE.call("tensor_tensor", out=qr[:, 0:nh, 1::2], in0=rt[2][:, 0:nh, :], in1=rt[3][:, 0:nh, :],
                      op=ALU.add)
        qf_res.set(td)
        cs_res.add(td)

        def do_tr(td=td, qr=qr, nh=nh, W=W, cg=cg, ts=ts, i=i):
            bk = nb()
            pe_take(bk)
            PE.wait(td)
            tok = None
            for hh in range(nh):
                tok = PE.call("transpose", out=bk.bf[:, hh * 128:(hh + 1) * 128], in_=qr[:, hh, :],
                              identity=ident_b[:], mark=(hh == nh - 1))
            qr_res4[ts].set(tok)
            si = qst_i[0] % 2
            qst_i[0] += 1
            ACT.wait(tok, qTst_res[si])
            te = ACT.call("activation", out=qT_st[si][:, 0:W], in_=bk.bf[:, 0:W], func=AF.Copy)
            bk.free = [te]
            POOL.wait(te)
            if cg < 2:
                tst = POOL.dma(qst_sem[si], qT_scr[cg, i * 4 + ts], qT_st[si][:])
            else:
                tst = POOL.dma(qst_sem[si],
                               kvl_k.rearrange("(g d) t -> d g t", g=2)[:, :, i * TT + ts * 128:i * TT + (ts + 1) * 128],
                               qT_st[si][:, 0:256].rearrange("p (g t) -> p g t", g=2))
            qTst_res[si].set(te)
            qTst_res[si].add(tst)

        pend_tr.append(do_tr)

    def flush_tr():
        while pend_tr:
            pend_tr.pop(0)()

    halo_v = halo_loc.rearrange("(j p) c -> j p c", p=128)

    def uy_epilogue(i, cg, bk, gtok):
        flush_tr()
        for mi in range(4):
            if cg < 5:
                j = (cg - 3) * 4 + mi
                si = ust_i[0] % 2
                ust_i[0] += 1
                ACT.wait(gtok, ust_res[si])
                ta = ACT.call("activation", out=ust[si][:], in_=bk[mi].ap, func=AF.Copy)
                bk[mi].free = [ta]
                POOL.wait(ta)
                tst = POOL.dma(ust_sem[si], uT_scr[j, :, i * TT:(i + 1) * TT], ust[si][:])
                ust_res[si].set(ta)
                ust_res[si].add(tst)
                if i == 0:
                    ust_res[si].add(POOL.dma(ust_sem[si], halo_v[j, :, 0:1], ust[si][:, 0:1]))
                if i == NTILE - 1:
                    ust_res[si].add(POOL.dma(ust_sem[si], halo_v[j, :, 1:3], ust[si][:, TT - 2:TT]))
            else:
                j = (cg - 5) * 4 + mi
                y0, y1, y2 = yt[0], yt[1], yt[2]
                ACT.wait(gtok, yt_res)
                ta = ACT.call("activation", out=y0[:], in_=bk[mi].ap, func=AF.Copy)
                bk[mi].free = [ta]
                DVE.wait(ta, yt_res)
                td = DVE.call("tensor_tensor", out=y1[:], in0=y0[:], in1=y0[:], op=ALU.mult)
                DVE.wait(td)
                td = DVE.call("tensor_scalar", out=y1[:], in0=y1[:], scalar1=0.044715, scalar2=1.0,
                              op0=ALU.mult, op1=ALU.add)
                DVE.wait(td)
                td = DVE.call("tensor_tensor", out=y1[:], in0=y1[:], in1=y0[:], op=ALU.mult)
                ACT.wait(td)
                ta = ACT.call("activation", out=y1[:], in_=y1[:], func=AF.Sigmoid, scale=GELU_K)
                DVE.wait(ta)
                td = DVE.call("tensor_tensor", out=y2[:], in0=y1[:], in1=y0[:], op=ALU.mult)
                POOL.wait(td)
                tst = POOL.dma(yt_sem, ygT_scr[j, :, i * TT:(i + 1) * TT], y2[:])
                yt_res.set(td)
                yt_res.add(tst)

    pipe2_units = []
    pipe2_cls = []
    try:
        x_tiles = x_in.rearrange("(i ts p) f -> i p ts f", ts=4, p=128)
        x1_tiles = x1_scr.rearrange("(i ts p) f -> i p ts f", ts=4, p=128)
        cs_tiles = cs_in.rearrange("(i ts p) c -> i p ts c", ts=4, p=128)

        save_off = aoff[0]
        aoff[0] = 0
        NSTG = 8
        stg_f = [carve(f"stgf{i}", [128, 2048], F32) for i in range(NSTG)]
        stg_b = [carve(f"stgb{i}", [128, 2048], BF16) for i in range(NSTG)]
        aoff[0] = save_off
        pre_units = []

        def add_up(dst, w1, w3):
            for cg in range(22):
                for kq in range(4):
                    lds = []
                    for wi, w in enumerate((w1, w3)):
                        src = w[kq * 512:(kq + 1) * 512, cg * 256:(cg + 1) * 256].rearrange("(k p) c -> p k c", p=128)
                        lds.append((lambda st, wi=wi: st.rearrange("p (k w c) -> p k w c", k=4, w=2)[:, :, wi, :], src))
                    pre_units.append((dst[cg * 4 + kq], lds, 2048))

        def add_T(dst, w, n_cg, n_kq, ku=4):
            for cg in range(n_cg):
                for kq in range(n_kq):
                    src = w[kq * ku * 128:(kq + 1) * ku * 128, cg * 512:(cg + 1) * 512].rearrange("(k p) c -> p k c", p=128)
                    lds = [(lambda st, ku=ku: st[:, 0:ku * 512].rearrange("p (k c) -> p k c", k=ku), src)]
                    pre_units.append((dst[cg * n_kq + kq][:, 0:ku * 512], lds, ku * 512))

        add_up(wup_a, w_f["w1a"], w_f["w3a"])
        add_T(wdn_a, w_f["w2a"], 4, 11)
        add_T(win_b, w_f["win"], 7, 4)
        for x_, src_ in enumerate((lwa_in, lwi_in)):
            pre_units.append((lw_b[x_].rearrange("(a p) o -> p a o", p=128),
                              [(lambda st: st.rearrange("p (a o) -> p a o", a=16), src_.rearrange("(a p) o -> p a o", p=128))],
                              2048))
        add_T(wout_b, w_f["wout"], 4, 4)
        add_up(wup_b, w_f["w1b"], w_f["w3b"])
        add_T(wdn_b, w_f["w2b"], 4, 11)
        add_T(wg_b, w_f["wg"], 4, 4)
        add_T(wpp_b, w_f["wpp"], 4, 1, ku=2)
        class PrePipe:
            def __init__(self, units_, stg_f_, stg_b_, tag, use_act):
                self.units = units_
                self.sf = stg_f_
                self.sb_ = stg_b_
                self.ns = len(stg_f_)
                self.depth = self.ns - 1
                self.ld_sem = [dsem(f"d_pl{tag}{i}") for i in range(self.ns)]
                self.st_sem = [dsem(f"d_ps{tag}{i}") for i in range(self.ns)]
                self.ld_tok = {}
                self.cast_rd = [None] * self.ns
                self.pst_tok = [None] * self.ns
                self.use_act = use_act
                self.u = 0
                self.total = len(units_) + self.depth

            def step(self):
                u = self.u
                if u >= self.total:
                    return False
                self.u += 1
                NU_ = len(self.units)
                if u < NU_:
                    sl = u % self.ns
                    SP.wait(self.cast_rd[sl])
                    for vf, src in self.units[u][1]:
                        self.ld_tok[u] = SP.dma(self.ld_sem[sl], vf(self.sf[sl]), src)
                v_ = u - self.depth
                if v_ >= 0:
                    sl = v_ % self.ns
                    dst, _, n_ = self.units[v_]
                    if v_ % 2 == 0 or not self.use_act:
                        DVE.wait(self.ld_tok.pop(v_), self.pst_tok[sl])
                        tk = DVE.call("tensor_copy", out=self.sb_[sl][:, 0:n_], in_=self.sf[sl][:, 0:n_])
                    else:
                        ACT.wait(self.ld_tok.pop(v_), self.pst_tok[sl])
                        tk = ACT.call("activation", out=self.sb_[sl][:, 0:n_], in_=self.sf[sl][:, 0:n_], func=AF.Copy)
                    self.cast_rd[sl] = tk
                    SP.wait(tk)
                    if len(dst.shape) == 3:
                        self.pst_tok[sl] = SP.dma(self.st_sem[sl], dst,
                                                  self.sb_[sl][:, 0:n_].rearrange("p (a o) -> p a o", a=dst.shape[1]))
                    else:
                        self.pst_tok[sl] = SP.dma(self.st_sem[sl], dst, self.sb_[sl][:, 0:n_])
                return True

            def drain(self):
                while self.step():
                    pass

        N_EARLY = 88 + 44 + 28 + 2
        pipe1 = PrePipe(pre_units[:N_EARLY], stg_f, stg_b, "a", True)
        pipe1.drain()
        pipe2_units.extend(pre_units[N_EARLY:])
        pipe2_cls.append(PrePipe)
        for k_ in ("upa", "dna", "win", "lw", "wout", "upb", "dnb", "wg", "wpp"):
            cast_tok[k_] = None
        late = []
        barrier()
        tx = POOL.dma(s_x, xt[:], x_tiles[0])
        xt_res.set(tx)

        DVE.wait(const_res)
        ACT.wait(const_res)
        PE.wait(const_res)
        tp = norm_front()
        norm_scale(tp)
        cp(1)
        for i in range(NTILE):
            norm_back(0, C_G1)
            cp(2)
            ffn(0, wup_a, wdn_a, cast_tok["upa"], cast_tok["dna"])
            cp(3)
            POOL.wait(xt_res)
            t_st = POOL.dma(s_st, x1_tiles[i], xt[:])
            tp = norm_front()
            norm_scale(tp)
            norm_back(1, C_GM)
            cp(4)
            SP.wait(cs_res)
            tcs = SP.dma(s_cs, cs_t[:], cs_tiles[i])
            cs_res.set(tcs)

            def qkv_epi(cg, bk, gtok, i=i):
                flush_tr()
                for ts in range(4):
                    qk_epilogue(i, cg, ts, bk[ts], gtok)

            lt = proj_T(lambda kc, ts: hT[1][:, kc, ts * 128:(ts + 1) * 128], hT_w[1], 16, win_b, 7,
                        cast_tok["win"], qkv_epi, cg_list=[0, 1, 2])
            hT_r[1].add(lt)
            cp(5)
            if i + 1 < NTILE:
                POOL.wait(t_st, xt_res)
                tx = POOL.dma(s_x, xt[:], x_tiles[i + 1])
                xt_res.set(tx)
                tp = norm_front()
                norm_scale(tp)
            lt = proj_F(lambda kc: hT[1][:, kc, :], hT_w[1], 16, win_b, [3, 4, 5, 6], cast_tok["win"],
                        lambda cg, bk, gtok, i=i: uy_epilogue(i, cg, bk, gtok), False)
            hT_r[1].add(lt)
            n_late = (len(late) + NTILE - 1) // NTILE
            for _ in range(n_late):
                if late:
                    late.pop(0)()
        while late:
            late.pop(0)()


    except StopEmit:
        pass

    groups = [[0, 1, 2, 3], [4, 5, 6, 7]]
    POOL.wait([Tok(d.sem, d.n) for d in all_store_sems if d.n > 0])
    if not NO_CC:
        for c_ in range(8):
            POOL.ops.append((1, lambda e, c_=c_: e.collective_compute(
                "AllGather", ALU.bypass, replica_groups=groups, ins=[kv_loc[c_ * 64:(c_ + 1) * 64, :]],
                outs=[kv_g[c_ * 256:(c_ + 1) * 256, :]]), (s_cc, 1)))
        POOL.ops.append((1, lambda e: e.collective_compute("AllGather", ALU.bypass, replica_groups=groups,
                                                           ins=[halo_loc2], outs=[halo_g2]), (s_cc, 1)))
        cc_tok = Tok(s_cc, 9)
    else:
        cc_tok = None
    if DBG and NTILE == 8:
        POOL.dma(s_st, dbg_kv, kv_loc)


    cc_state[0] = 9 if not NO_CC else 0

    def allgather(src, dst, rows):
        if not NO_CC:
            POOL.ops.append((1, lambda e: e.collective_compute("AllGather", ALU.bypass, replica_groups=groups,
                                                               ins=[src], outs=[dst]), (s_cc, 1)))
            cc_state[0] += 1
        else:
            for r in range(4):
                POOL.dma(s_st, dst[r * rows:(r + 1) * rows, :], src)

    if NO_CC:
        for r in range(4):
            for c_ in range(8):
                POOL.dma(s_st, kv_g[c_ * 256 + r * 64:c_ * 256 + (r + 1) * 64, :], kv_loc[c_ * 64:(c_ + 1) * 64, :])
            POOL.dma(s_st, halo_g[r * 1024:(r + 1) * 1024, :], halo_loc)
    barrier()

    if 'C' in PHASES:
        aoff[0] = 0
        kT_sb = carve("kT_sb", [128, SEQ], BF16)
        v_sb = carve("v_sb", [128, 128, 128], BF16)
        qT_sb = [carve("qTsb0", [128, 512], BF16), carve("qTsb1", [128, 512], BF16)]
        NP = 3
        Pr = [carve(f"P{i}", [128, 1024], BF16) for i in range(NP)]
        Ps = [carve(f"Ps{i}", [128, 512], BF16) for i in range(NP)]
        rl = carve("rl", [128, 512], F32)
        oT = [carve("oT0", [128, 512], BF16), carve("oT1", [128, 512], BF16)]
        s_kv = dsem("d_kv")
        s_q = [dsem("d_q0"), dsem("d_q1")]
        s_o = [stsem("d_o0"), stsem("d_o1")]
        NQT = NTILE * 4
        NPAIR = NKC // 2
        SCALE = 1.0 / math.sqrt(128.0)
        units = [(g, qt, pr) for g in range(2) for qt in range(NQT) for pr in range(NPAIR)]
        q_load_tok = {}
        q_free = [None, None]
        kv_tok = {}
        kv_free = Res()
        S_free = [None, None]
        P_free = [None] * NP
        tokE = {}
        tokD = {}
        oT_res = [Res(), Res()]

        def load_q(g, qt):
            ti = g * NQT + qt
            SP.wait(q_free[ti % 2])
            q_load_tok[(g, qt)] = SP.dma(s_q[ti % 2], qT_sb[ti % 2][:], qT_scr[g, qt])

        def load_kv(g):
            SP.wait(kv_free)
            for r in range(4):
                for hf_ in range(2):
                    c1 = 2 * g + hf_
                    SP.dma(s_kv, kT_sb[hf_ * 64:(hf_ + 1) * 64, r * 4096:(r + 1) * 4096],
                           kv_g[c1 * 256 + r * 64:c1 * 256 + (r + 1) * 64, :])
                    c2 = 4 + 2 * g + hf_
                    vv = kv_g[c2 * 256 + r * 64:c2 * 256 + (r + 1) * 64, :].rearrange("p (b d) -> p b d", d=128)
                    kv_tok[g] = SP.dma(s_kv, v_sb[hf_ * 64:(hf_ + 1) * 64, r * 32:(r + 1) * 32, :], vv)

        def emit_S(n):
            g, qt, pr = units[n]
            ti = g * NQT + qt
            if pr == 0:
                if qt == 0:
                    load_kv(g)
                    load_q(g, 0)
                if qt + 1 < NQT:
                    load_q(g, qt + 1)
                PE.wait(q_load_tok[(g, qt)], kv_tok[g])
            PE.wait(S_free[n % 2])
            tok = None
            for j in range(2):
                kc = 2 * pr + j
                tok = PE.call("matmul", dbk[n % 2][:, j * 512:(j + 1) * 512], kT_sb[:, kc * 128:(kc + 1) * 128],
                              qT_sb[ti % 2][:], start=True, stop=True, mark=(j == 1))
            if pr == NPAIR - 1:
                q_free[ti % 2] = tok
                if qt == NQT - 1:
                    kv_free.add(tok)
            ACT.wait(tok, P_free[n % NP])
            te = ACT.call("activation", out=Pr[n % NP][:], in_=dbk[n % 2][:, 0:1024], func=AF.Exp, scale=SCALE)
            S_free[n % 2] = te
            tokE[n] = te
            DVE.wait(te, P_free[n % NP])
            tokD[n] = DVE.call("tensor_tensor", out=Ps[n % NP][:], in0=Pr[n % NP][:, 0:512], in1=Pr[n % NP][:, 512:1024],
                               op=ALU.add)

        def emit_PV(n):
            g, qt, pr = units[n]
            ti = g * NQT + qt
            a = ti % 2
            O, L = banks[4 + 2 * a], banks[5 + 2 * a]
            if pr == 0:
                pe_take(O)
                pe_take(L)
            PE.wait(tokE.pop(n))
            tok = None
            for j in range(2):
                kc = 2 * pr + j
                first = (pr == 0 and j == 0)
                last = (pr == NPAIR - 1 and j == 1)
                PE.call("matmul", O.ap, v_sb[:, kc, :], Pr[n % NP][:, j * 512:(j + 1) * 512], start=first, stop=last,
                        mark=False)
            PE.wait(tokD.pop(n))
            tok = PE.call("matmul", L.ap, ones_b[:], Ps[n % NP][:], start=(pr == 0), stop=(pr == NPAIR - 1))
            P_free[n % NP] = tok
            if pr == NPAIR - 1:
                if qt == NQT - 1:
                    kv_free.add(tok)
                DVE.wait(tok)
                td = DVE.call("reciprocal", out=rl[:], in_=L.ap)
                L.free = [td]
                DVE.wait(oT_res[a])
                td2 = DVE.call("tensor_tensor", out=oT[a][:], in0=O.ap, in1=rl[:], op=ALU.mult)
                O.free = [td2]
                POOL.wait(td2)
                tst = POOL.dma(s_o[a], mixT_scr[g * 4:(g + 1) * 4, :, qt * 128:(qt + 1) * 128].rearrange("h d t -> d h t"),
                               oT[a][:].rearrange("p (h t) -> p h t", h=4))
                oT_res[a].set(tst)

        for b_ in banks:
            b_.free = []
        ACT.serial = False
        pipe2 = None
        if pipe2_cls:
            stg_f2 = [carve(f"stgf2{i}", [128, 2048], F32) for i in range(4)]
            stg_b2 = [carve(f"stgb2{i}", [128, 2048], BF16) for i in range(4)]
            pipe2 = pipe2_cls[0](pipe2_units, stg_f2, stg_b2, "b", False)
        for n in range(len(units)):
            if n >= 1 and units[n][1] == 0 and units[n][2] == 0:
                emit_PV(n - 1)
                emit_S(n)
            else:
                emit_S(n)
                if n >= 1:
                    emit_PV(n - 1)
            if pipe2 is not None:
                while pipe2.u * len(units) < (n + 1) * pipe2.total:
                    if not pipe2.step():
                        break
        emit_PV(len(units) - 1)
        ACT.serial = True
        if pipe2 is not None:
            pipe2.drain()
        barrier()


    if 'D' in PHASES:
        aoff[0] = 0
        us = carve("us", [128, NTOK + 8], F32)
        uc = carve("uc", [128, NTOK], F32)
        ucb = carve("ucb", [128, NTOK], BF16)
        rb = carve("rb", [128, NTOK], F32)
        ig = carve("ig", [128, NTOK], F32)
        mb = carve("mb", [128, NTOK], F32)
        hf = carve("hf", [128, NTOK], F32)
        lw_sb = carve("lw_sb", [128, 2, 16, 128], BF16)
        hlt = carve("hlt", [128, 4, 8, 8], F32)
        sgt = carve("sgt", [128, 4, 2, 8, 8], F32)
        sumst = carve("sumst", [128, 2, 8, 8], F32)
        hin = carve("hin", [128, 8], F32)
        rsum = carve("rsum", [128, 8], F32)
        s_d = dsem("d_lru")
        s_dg = dsem("d_lrug")
        s_du = dsem("d_lruu")
        s_dy = dsem("d_lruy")
        s_dst = stsem("d_lrust")
        NT8 = NTOK // TT
        hb = us[:, 0:NTOK]

        def dserial(*engs):
            toks = [Tok(S.sem, S.count) for S in engs if S.sem is not None and S.count > 0]
            for S in engs:
                S.wait(toks)

        mask = lambda k, r: smalls[:, C_MASK + k * 4 + r:C_MASK + k * 4 + r + 1]
        t0_ = SP.dma(s_d, lw_sb[:, 0], lw_b[0].rearrange("(a p) o -> p a o", p=128))
        t0_ = SP.dma(s_d, lw_sb[:, 1], lw_b[1].rearrange("(a p) o -> p a o", p=128))
        for r in range(4):
            t0_ = SP.dma(s_d, hlt[:, r], halo_g[r * 1024:(r + 1) * 1024, :].rearrange("(j p) c -> p j c", p=128))
        DVE.wait(t0_)
        DVE.call("memset", sumst[:], 0.0)

        def lru_pass(ps):
            if ps == 1:
                tl = None
                for r in range(4):
                    for d_ in range(2):
                        tl = SP.dma(s_dg, sgt[:, r, d_], sum_g[r * 2048 + d_ * 1024:r * 2048 + (d_ + 1) * 1024, :]
                                    .rearrange("(j p) c -> p j c", p=128))
                DVE.wait(tl)
            for j in range(8):
                dserial(SP, ACT, DVE, POOL, PE)
                SP.wait([Tok(S.sem, S.count) for S in (ACT, DVE, POOL, PE)])
                SP.wait(Tok(s_du.sem, s_du.n))
                tl = SP.dma(s_du, us[:, 2:2 + NTOK], uT_scr[j])
                DVE.wait(tl, t0_)
                PE.wait(t0_)
                DVE.call("tensor_scalar", out=us[:, 0:2], in0=hlt[:, 0, j, 1:3], scalar1=mask(2, 0), scalar2=None, op0=ALU.mult)
                DVE.call("tensor_scalar", out=us[:, 2 + NTOK:3 + NTOK], in0=hlt[:, 0, j, 0:1], scalar1=mask(3, 0), scalar2=None,
                         op0=ALU.mult)
                for r in range(1, 4):
                    DVE.call("scalar_tensor_tensor", out=us[:, 0:2], in0=hlt[:, r, j, 1:3], scalar=mask(2, r), in1=us[:, 0:2],
                             op0=ALU.mult, op1=ALU.add)
                    DVE.call("scalar_tensor_tensor", out=us[:, 2 + NTOK:3 + NTOK], in0=hlt[:, r, j, 0:1], scalar=mask(3, r),
                             in1=us[:, 2 + NTOK:3 + NTOK], op0=ALU.mult, op1=ALU.add)
                cw = lambda tap: smalls[:, C_CW + tap * 8 + j:C_CW + tap * 8 + j + 1]
                DVE.call("tensor_scalar", out=uc[:], in0=us[:, 0:NTOK], scalar1=cw(0), scalar2=smalls[:, C_CB + j:C_CB + j + 1],
                         op0=ALU.mult, op1=ALU.add)
                for tap in range(1, 4):
                    DVE.call("scalar_tensor_tensor", out=uc[:], in0=us[:, tap:tap + NTOK], scalar=cw(tap), in1=uc[:],
                             op0=ALU.mult, op1=ALU.add)
                dserial(ACT, DVE)
                ACT.call("activation", out=ucb[:], in_=uc[:], func=AF.Copy)
                dserial(ACT, PE)
                for d_ in range(2):
                    cd = cdec[:, d_ * 8 + j:d_ * 8 + j + 1]
                    for x_, dst, cb in ((0, rb, C_BA), (1, ig, C_BI)):
                        for tt in range(NT8):
                            bk = nb()
                            pe_take(bk)
                            tok = PE.call("matmul", bk.ap, lw_sb[:, x_, d_ * 8 + j, :], ucb[:, tt * TT:(tt + 1) * TT],
                                          start=True, stop=True)
                            ACT.wait(tok)
                            ta = ACT.call("activation", out=dst[:, tt * TT:(tt + 1) * TT], in_=bk.ap, func=AF.Sigmoid,
                                          bias=smalls[:, cb + d_ * 8 + j:cb + d_ * 8 + j + 1])
                            bk.free = [ta]
                    dserial(ACT, DVE, PE)
                    if ps == 0:
                        DVE.call("reduce_sum", out=rsum[:, 0:1], in_=rb[:], axis=AX.X)
                        dserial(ACT, DVE)
                        ACT.call("activation", out=sumst[:, d_, j, 0:1], in_=rsum[:, 0:1], func=AF.Exp, scale=cd)
                    ACT.call("activation", out=mb[:], in_=rb[:], func=AF.Exp, scale=cdec2[:, d_ * 8 + j:d_ * 8 + j + 1])
                    ACT.call("activation", out=rb[:], in_=rb[:], func=AF.Exp, scale=cd)
                    ACT.call("activation", out=mb[:], in_=mb[:], func=AF.Sqrt, scale=-1.0, bias=1.0)
                    dserial(ACT, DVE)
                    DVE.call("tensor_tensor", out=mb[:], in0=mb[:], in1=ig[:], op=ALU.mult)
                    DVE.call("tensor_tensor", out=mb[:], in0=mb[:], in1=uc[:], op=ALU.mult)
                    hdst = hf if d_ == 0 else hb
                    if ps == 0:
                        init = 0.0
                    else:
                        hcol = hin[:, d_:d_ + 1]
                        tcol = hin[:, 4:5]
                        DVE.call("memset", hcol, 0.0)
                        order = range(4) if d_ == 0 else range(3, -1, -1)
                        for r in order:
                            A_r = sgt[:, r, d_, j, 0:1]
                            B_r = sgt[:, r, d_, j, 1:2]
                            DVE.call("scalar_tensor_tensor", out=tcol, in0=hcol, scalar=A_r, in1=B_r, op0=ALU.mult, op1=ALU.add)
                            DVE.call("tensor_tensor", out=tcol, in0=tcol, in1=hcol, op=ALU.subtract)
                            DVE.call("scalar_tensor_tensor", out=hcol, in0=tcol, scalar=mask(d_, r), in1=hcol,
                                     op0=ALU.mult, op1=ALU.add)
                        init = hcol
                    if d_ == 0:
                        DVE.call("tensor_tensor_scan", out=hdst[:, 0:NTOK], data0=rb[:], data1=mb[:], initial=init,
                                 op0=ALU.mult, op1=ALU.add)
                        if ps == 0:
                            DVE.call("tensor_copy", out=sumst[:, d_, j, 1:2], in_=hdst[:, NTOK - 1:NTOK])
                    else:
                        DVE.call("tensor_tensor_scan", out=hdst[:, ::-1], data0=rb[:, ::-1], data1=mb[:, ::-1], initial=init,
                                 op0=ALU.mult, op1=ALU.add)
                        if ps == 0:
                            DVE.call("tensor_copy", out=sumst[:, d_, j, 1:2], in_=hdst[:, 0:1])
                    dserial(ACT, DVE, PE)
                if ps == 1:
                    dserial(SP, ACT, DVE)
                    SP.wait(Tok(DVE.sem, DVE.count))
                    SP.wait(Tok(s_dy.sem, s_dy.n))
                    tl = SP.dma(s_dy, ig[:], ygT_scr[j])
                    DVE.wait(tl)
                    DVE.call("tensor_tensor", out=hf[:], in0=hf[:], in1=hb, op=ALU.add)
                    td = DVE.call("tensor_tensor", out=ucb[:], in0=hf[:], in1=ig[:], op=ALU.mult)
                    POOL.wait(td)
                    tst = POOL.dma(s_dst, mixT_scr[8 + j], ucb[:])
                    DVE.wait(tst)
                    ACT.wait(tst)
            if ps == 0:
                POOL.wait(Tok(DVE.sem, DVE.count), Tok(ACT.sem, ACT.count))
                tst = POOL.dma(s_dst, sum_loc.rearrange("(d j p) c -> p d j c", d=2, j=8), sumst[:])
                POOL.wait(tst)
                allgather(sum_loc2 if not NO_CC else sum_loc, sum_g2 if not NO_CC else sum_g, 2048)
                barrier()

        lru_pass(0)
        lru_pass(1)
        barrier()


    if 'E' in PHASES:
        aoff[0] = aoff_AE
        pt_f = carve("pt_f", [128, 4, 256], F32)
        pb = carve("pb", [128, 4, 256], BF16)
        pT = carve("pT", [128, 2, TT], BF16)
        s_p = dsem("d_p")
        s_m = dsem("d_m")
        out_tiles = out.rearrange("(i ts p) f -> i p ts f", ts=4, p=128)
        p_tiles = p_in.rearrange("(i ts p) f -> i p ts f", ts=4, p=128)
        for R_ in (xt_res, xb_res, junk_res, aT_r, aT_w, hT_w[0], hT_w[1], hT_r[0], hT_r[1]):
            R_.t = {}
        for b_ in banks:
            b_.free = []
        ring_free[:] = [None] * NR
        silu_free[:] = [None, None]
        t_out = None
        for i in range(NTILE):
            POOL.wait(xt_res, t_out)
            tx = POOL.dma(s_x, xt[:], x1_tiles[i])
            xt_res.set(tx)
            SP.wait(hT_r[1])
            tm = SP.dma(s_m, hT[1][:], mixT_scr[:, :, i * TT:(i + 1) * TT].rearrange("c d t -> d c t"))
            hT_w[1] = Res()
            hT_w[1].add(tm)
            SP.wait(Tok(DVE.sem, DVE.count))
            tpl = SP.dma(s_p, pt_f[:], p_tiles[i])

            def o_epi(fg, bk, gtok):
                for ts in range(4):
                    DVE.wait(gtok, xt_res)
                    sl = xt[:, ts, fg * 512:(fg + 1) * 512]
                    td = DVE.call("tensor_tensor", out=sl, in0=bk[ts].ap, in1=sl, op=ALU.add)
                    bk[ts].free = [td]
                    xt_res.add(td)

            lt = proj_T(lambda kc, ts: hT[1][:, kc, ts * 128:(ts + 1) * 128], hT_w[1], 16, wout_b, 4, cast_tok["wout"], o_epi)
            hT_r[1].add(lt)
            tp = norm_front()
            norm_scale(tp)
            norm_back(0, C_G2)
            ffn(0, wup_b, wdn_b, cast_tok["upb"], cast_tok["dnb"])
            tp = norm_front()
            norm_scale(tp)
            norm_back(1, C_GP)
            DVE.wait(tpl)
            tdp = DVE.call("tensor_copy", out=pb[:], in_=pt_f[:])
            pT_tok = None
            for kc in range(2):
                bk = nb()
                pe_take(bk)
                PE.wait(tdp)
                tok = None
                for ts in range(4):
                    tok = PE.call("transpose", out=bk.bf[:, ts * 128:(ts + 1) * 128], in_=pb[:, ts, kc * 128:(kc + 1) * 128],
                                  identity=ident_b[:], mark=(ts == 3))
                DVE.wait(tok)
                pT_tok = DVE.call("tensor_copy", out=pT[:, kc, :], in_=bk.bf[:, 0:TT])
                bk.free = [pT_tok]
            for fg in range(4):
                gate_bk = []

                def g_epi(cg, bk, gtok):
                    gate_bk.append((bk, gtok))

                lt = proj_T(lambda kc, ts: hT[1][:, kc, ts * 128:(ts + 1) * 128], hT_w[1], 16, wg_b, 4, cast_tok["wg"], g_epi,
                            cg_list=[fg])
                hT_r[1].add(lt)

                def pp_epi(cg, bk, gtok):
                    gb, gt = gate_bk[0]
                    for ts in range(4):
                        ACT.wait(gt)
                        ta = ACT.call("activation", out=silu_tmp[0][:], in_=gb[ts].ap, func=AF.Sigmoid)
                        gb[ts].free = [ta]
                        DVE.wait(ta, gtok)
                        td = DVE.call("tensor_tensor", out=silu_tmp[1][:], in0=silu_tmp[0][:], in1=bk[ts].ap, op=ALU.mult)
                        bk[ts].free = [td]
                        ACT.wait(td)
                        sl = xt[:, ts, cg * 512:(cg + 1) * 512]
                        td = DVE.call("tensor_tensor", out=sl, in0=sl, in1=silu_tmp[1][:], op=ALU.add)
                        xt_res.add(td)

                proj_T(lambda kc, ts: pT[:, kc, ts * 128:(ts + 1) * 128], pT_tok, 2, wpp_b, 4, cast_tok["wpp"], pp_epi, ku=2,
                       cg_list=[fg])
            tp = norm_front()
            for ts in range(4):
                DVE.wait(tp, xt_res)
                td = DVE.call("scalar_tensor_tensor", out=xt[:, ts, :], in0=xt[:, ts, :], scalar=rstd[:, ts:ts + 1],
                              in1=rows[:, 0:D], op0=ALU.mult, op1=ALU.mult)
                xt_res.add(td)
            POOL.wait(xt_res)
            t_out = POOL.dma(s_st, out_tiles[i], xt[:])
            xt_res.add(t_out)


    POOL.wait([Tok(d.sem, d.n) for d in all_store_sems if d.n > 0])

    with nc.allow_non_contiguous_dma(reason="tiny halo / edge transfers"), nc.Block() as block:
        @block.sync
        def _(e):
            SP.replay(e)

        @block.scalar
        def _(e):
            ACT.replay(e)

        @block.vector
        def _(e):
            DVE.replay(e)

        @block.gpsimd
        def _(e):
            POOL.replay(e)

        @block.tensor
        def _(e):
            PE.replay(e)
    es.close()
    return nc


def _pc(v, nchunk):
    return np.ascontiguousarray(np.asarray(v, np.float32).reshape(nchunk, 128).T)


def _prep_inputs(inp):
    f = lambda k: np.ascontiguousarray(np.asarray(inp[k], np.float32))
    x = f("x")
    p = f("p")[0]
    shared = {
        "w1a": f("w1_ffn1")[0], "w3a": f("w3_ffn1")[0], "w2a": f("w2_ffn1")[0],
        "win": f("w_in")[0], "wout": f("w_out")[0],
        "w1b": f("w1_ffn2")[0], "w3b": f("w3_ffn2")[0], "w2b": f("w2_ffn2")[0],
        "wg": f("w_ple_gate")[0], "wpp": f("w_ple_proj")[0],
        "lwa": f("lru_wa")[0].reshape(2048, 128), "lwi": f("lru_wi")[0].reshape(2048, 128),
        "ident": np.eye(128, dtype=np.float32),
    }
    rows = np.concatenate([f("norm_final"), f("q_norm")[0], f("k_norm")[0]])[None, :]
    shared["rows"] = np.ascontiguousarray(np.broadcast_to(rows, (128, 2304)))
    sm = np.zeros((128, NS), np.float32)
    sm[:, C_G1:C_G1 + 16] = _pc(f("norm_ffn1")[0], 16)
    sm[:, C_GM:C_GM + 16] = _pc(f("norm_mix")[0], 16)
    sm[:, C_G2:C_G2 + 16] = _pc(f("norm_ffn2")[0], 16)
    sm[:, C_GP:C_GP + 16] = _pc(f("norm_ple")[0], 16)
    cw = f("conv_w")[0]
    for tap in range(4):
        sm[:, C_CW + tap * 8:C_CW + tap * 8 + 8] = _pc(cw[tap], 8)
    sm[:, C_CB:C_CB + 8] = _pc(f("conv_b")[0], 8)
    sm[:, C_BA:C_BA + 16] = _pc(f("lru_ba")[0].reshape(-1), 16)
    sm[:, C_BI:C_BI + 16] = _pc(f("lru_bi")[0].reshape(-1), 16)
    sm[:, C_LAM:C_LAM + 16] = _pc(f("lru_lambda")[0].reshape(-1), 16)
    t = np.arange(SEQ)
    inv = np.float32(10000.0) ** (-np.arange(0, 64, 2, dtype=np.float32) / np.float32(64))
    ang = np.concatenate([(t // 64).astype(np.float32)[:, None] * inv,
                          (t % 64).astype(np.float32)[:, None] * inv], axis=-1).astype(np.float32)
    cs_full = np.concatenate([np.cos(ang), np.sin(ang)], axis=-1).astype(np.float32)
    in_maps = []
    for c in range(8):
        b, sc = c // 4, c % 4
        m = dict(shared)
        m["x"] = np.ascontiguousarray(x[b, sc * NTOK:(sc + 1) * NTOK])
        m["p"] = np.ascontiguousarray(p[b, sc * NTOK:(sc + 1) * NTOK])
        m["cs"] = np.ascontiguousarray(cs_full[sc * NTOK:(sc + 1) * NTOK])
        s2 = sm.copy()
        for r in range(4):
            s2[:, C_MASK + r] = 1.0 if r < sc else 0.0
            s2[:, C_MASK + 4 + r] = 1.0 if r > sc else 0.0
            s2[:, C_MASK + 8 + r] = 1.0 if r == sc - 1 else 0.0
            s2[:, C_MASK + 12 + r] = 1.0 if r == sc + 1 else 0.0
        m["smalls"] = s2
        in_maps.append(m)
    return in_maps


def kernel(**inputs):
    in_maps = _prep_inputs(inputs)
    nc = build_nc()
    res = run_bass_kernel_spmd(nc, in_maps, core_ids=list(range(8)))
    outs = [np.asarray(r["out"]) for r in res.results]
    full = np.stack([np.concatenate(outs[0:4], axis=0), np.concatenate(outs[4:8], axis=0)], axis=0)
    return full.astype(np.float32)
```
